# Optimizing a Trainium2 kernel written in Bass

```python
import jax, jax.numpy as jnp
from jax import lax
import numpy as np

D_MODEL = 1024
BATCH = 8
SEQ = 2048
DEPTH = 4
DEC_BATCH = 128
DEC_SEQ = 8
PAST_LEN = 16384
PAGE_SIZE = 128

N_MIXERS = 2
N_CONV_LAYERS = (DEPTH + 1) // 2
N_SGU_LAYERS = DEPTH // 2
D_CONV = D_MODEL
CONV_W = 3
D_SGU = 3 * D_MODEL
N_SGU_HEADS = 8
CHUNK = 128
D_FF = 2816
ALPHA = (2.0 * DEPTH) ** 0.25
BETA = (8.0 * DEPTH) ** -0.25
LN_EPS = 1e-5

kernel_name = 'hybrid_shortconv_chunkgmlp_macaron_deepnorm_step'


def layer_norm(x, g, b):
    xf = x.astype(jnp.float32)
    mu = jnp.mean(xf, axis=-1, keepdims=True)
    var = jnp.mean(jnp.square(xf - mu), axis=-1, keepdims=True)
    return ((xf - mu) * lax.rsqrt(var + LN_EPS) * g.astype(jnp.float32) + b.astype(jnp.float32)).astype(x.dtype)


def post_norm(x, sub, g, b):
    return layer_norm(ALPHA * x + sub, g, b)


def swiglu(x, w_in, w_out):
    gate, up = jnp.split(x @ w_in, 2, axis=-1)
    return (jax.nn.silu(gate) * up) @ w_out


def conv_mixer(x, buf, w_in, w_conv, w_out):
    b_gate, c_gate, h = jnp.split(x @ w_in, 3, axis=-1)
    g = c_gate * h
    gp = jnp.concatenate([buf.astype(g.dtype), g], axis=1)
    conv = lax.conv_general_dilated(
        gp, w_conv[:, None, :].astype(g.dtype), window_strides=(1,), padding='VALID',
        dimension_numbers=('NWC', 'WIO', 'NWC'), feature_group_count=D_CONV)
    y = (b_gate * conv) @ w_out
    return y, gp[:, -(CONV_W - 1):, :]


def chunk_mix(v, w_s, b_s):
    n, L, dv = v.shape
    lc = L if L < CHUNK else CHUNK
    n_chunks = -(-L // lc)
    pad = n_chunks * lc - L
    vp = jnp.pad(v, ((0, 0), (0, pad), (0, 0))) if pad else v
    vc = vp.reshape(n, n_chunks, lc, N_SGU_HEADS, dv // N_SGU_HEADS)
    ws = jnp.tril(w_s[:, :lc, :lc]).astype(v.dtype)
    bs = jnp.transpose(b_s[:, :lc]).astype(v.dtype)
    out = jnp.einsum('hts,bcshd->bcthd', ws, vc) + bs[:, :, None]
    return out.reshape(n, n_chunks * lc, dv)[:, :L, :]


def sgu_mixer(x, w_in, ln_g, ln_b, w_s, b_s, w_out):
    z = jax.nn.gelu(x @ w_in, approximate=False)
    u, v = jnp.split(z, 2, axis=-1)
    v = layer_norm(v, ln_g, ln_b)
    y = (u * chunk_mix(v, w_s, b_s)) @ w_out
    return y, v


def setup_inputs(seed: int = 0) -> dict:
    key = jax.random.key(seed)
    ks = jax.random.split(key, 18)

    def nrm(k, shape, scale):
        return jax.random.normal(k, shape, jnp.float32) * scale

    return {
        'x_prompt': nrm(ks[0], (BATCH, SEQ, D_MODEL), 1.0),
        'x_sample': nrm(ks[1], (DEC_BATCH, DEC_SEQ, D_MODEL), 1.0),
        'cache_conv': nrm(ks[2], (N_CONV_LAYERS, DEC_BATCH, CONV_W - 1, D_CONV), 1.0),
        'ln_g': 1.0 + nrm(ks[3], (DEPTH, 3, D_MODEL), 0.02),
        'ln_b': nrm(ks[4], (DEPTH, 3, D_MODEL), 0.02),
        'ffn1_w_in': nrm(ks[5], (DEPTH, D_MODEL, 2 * D_FF), D_MODEL ** -0.5),
        'ffn1_w_out': nrm(ks[6], (DEPTH, D_FF, D_MODEL), BETA * D_FF ** -0.5),
        'ffn2_w_in': nrm(ks[7], (DEPTH, D_MODEL, 2 * D_FF), D_MODEL ** -0.5),
        'ffn2_w_out': nrm(ks[8], (DEPTH, D_FF, D_MODEL), BETA * D_FF ** -0.5),
        'conv_w_in': nrm(ks[9], (N_CONV_LAYERS, D_MODEL, 3 * D_CONV), D_MODEL ** -0.5),
        'conv_w': nrm(ks[10], (N_CONV_LAYERS, CONV_W, D_CONV), CONV_W ** -0.5),
        'conv_w_out': nrm(ks[11], (N_CONV_LAYERS, D_CONV, D_MODEL), BETA * D_CONV ** -0.5),
        'sgu_w_in': nrm(ks[12], (N_SGU_LAYERS, D_MODEL, 2 * D_SGU), D_MODEL ** -0.5),
        'sgu_ln_g': 1.0 + nrm(ks[13], (N_SGU_LAYERS, D_SGU), 0.02),
        'sgu_ln_b': nrm(ks[14], (N_SGU_LAYERS, D_SGU), 0.02),
        'sgu_w_s': nrm(ks[15], (N_SGU_LAYERS, N_SGU_HEADS, CHUNK, CHUNK), CHUNK ** -0.5),
        'sgu_b_s': 1.0 + nrm(ks[16], (N_SGU_LAYERS, N_SGU_HEADS, CHUNK), 0.02),
        'sgu_w_out': nrm(ks[17], (N_SGU_LAYERS, D_SGU, D_MODEL), BETA * D_SGU ** -0.5),
    }


def reference(x_prompt, x_sample, cache_conv, ln_g, ln_b, ffn1_w_in, ffn1_w_out, ffn2_w_in, ffn2_w_out,
              conv_w_in, conv_w, conv_w_out, sgu_w_in, sgu_ln_g, sgu_ln_b, sgu_w_s, sgu_b_s, sgu_w_out):
    xp, xs = x_prompt, x_sample
    conv_p, conv_s, sgu_s = [], [], []
    for i in range(DEPTH):
        j = i // N_MIXERS
        xp = post_norm(xp, 0.5 * swiglu(xp, ffn1_w_in[i], ffn1_w_out[i]), ln_g[i, 0], ln_b[i, 0])
        xs = post_norm(xs, 0.5 * swiglu(xs, ffn1_w_in[i], ffn1_w_out[i]), ln_g[i, 0], ln_b[i, 0])
        if i % N_MIXERS == 0:
            zero_buf = jnp.zeros((xp.shape[0], CONV_W - 1, D_CONV), xp.dtype)
            yp, bp = conv_mixer(xp, zero_buf, conv_w_in[j], conv_w[j], conv_w_out[j])
            ys, bs = conv_mixer(xs, cache_conv[j], conv_w_in[j], conv_w[j], conv_w_out[j])
            conv_p.append(bp)
            conv_s.append(bs)
        else:
            yp, _ = sgu_mixer(xp, sgu_w_in[j], sgu_ln_g[j], sgu_ln_b[j], sgu_w_s[j], sgu_b_s[j], sgu_w_out[j])
            ys, vs = sgu_mixer(xs, sgu_w_in[j], sgu_ln_g[j], sgu_ln_b[j], sgu_w_s[j], sgu_b_s[j], sgu_w_out[j])
            sgu_s.append(vs)
        xp = post_norm(xp, yp, ln_g[i, 1], ln_b[i, 1])
        xs = post_norm(xs, ys, ln_g[i, 1], ln_b[i, 1])
        xp = post_norm(xp, 0.5 * swiglu(xp, ffn2_w_in[i], ffn2_w_out[i]), ln_g[i, 2], ln_b[i, 2])
        xs = post_norm(xs, 0.5 * swiglu(xs, ffn2_w_in[i], ffn2_w_out[i]), ln_g[i, 2], ln_b[i, 2])
    return (xp, xs, jnp.stack(conv_p), jnp.stack(conv_s), jnp.stack(sgu_s))
```

```python
import numpy as np
from contextlib import ExitStack
import concourse.bass as bass
import concourse.mybir as mybir
from concourse.bass_utils import run_bass_kernel_spmd

F32 = mybir.dt.float32
BF16 = mybir.dt.bfloat16
AF = mybir.ActivationFunctionType
ALU = mybir.AluOpType

D = 1024
DEPTH = 4
DFF = 2816
NJ = DFF // 128
DSGU = 3072
ALPHA = (2.0 * DEPTH) ** 0.25
EPS = 1e-5
NCORES = 8
MAXNT = 9
MAXTOK = MAXNT * 128

DEFER_N = 26.0
GATE_ENG = "pool"
DEBUG_STOP = None


class Prog:
    ENGS = ("pe", "act", "dve", "pool", "sp")

    def __init__(self, nc, stack):
        self.nc = nc
        self.stack = stack
        self.streams = {e: [] for e in self.ENGS}
        self.sems = {}
        self.cnt = {}
        self.is_dma_sem = set()
        self.last_w = {}
        self.readers = {}
        self.base = {}
        self.pending = []
        for e in self.ENGS:
            self._sem("E_" + e)

    def defer(self, countdown, write_keys, emit_fn):
        self.pending.append([countdown, set(write_keys), emit_fn])

    def flush(self, upto=None):
        n = len(self.pending) if upto is None else upto + 1
        todo, self.pending = self.pending[:n], self.pending[n:]
        for _, _, fn in todo:
            fn()

    def _check_pending(self, reads, writes):
        if not self.pending:
            return
        keys = set(reads) | set(writes)
        hit = -1
        for idx, (_, wk, _) in enumerate(self.pending):
            if wk & keys:
                hit = idx
        if hit >= 0:
            self.flush(hit)

    def _sem(self, name):
        if name not in self.sems:
            self.sems[name] = self.stack.enter_context(self.nc.semaphore(name))
            self.cnt[name] = 0
        return self.sems[name]

    def _deps(self, reads, writes):
        deps = {}

        def add(tok):
            if tok is None:
                return
            s, v = tok
            if deps.get(s, 0) < v:
                deps[s] = v

        for k in reads:
            add(self.last_w.get(k))
            for s, v in self.base.get(k, {}).items():
                add((s, v))
        for k in writes:
            add(self.last_w.get(k))
            for s, v in self.readers.get(k, {}).items():
                add((s, v))
            for s, v in self.base.get(k, {}).items():
                add((s, v))
        return deps

    def _commit(self, tok, reads, writes):
        s, v = tok
        for k in reads:
            r = self.readers.setdefault(k, {})
            if r.get(s, 0) < v:
                r[s] = v
        for k in writes:
            self.last_w[k] = tok
            self.readers[k] = {}
            self.base.pop(k, None)

    def fence(self, k):
        b = dict(self.base.get(k, {}))
        lw = self.last_w.pop(k, None)
        if lw is not None:
            s, v = lw
            b[s] = max(b.get(s, 0), v)
        for s, v in self.readers.pop(k, {}).items():
            b[s] = max(b.get(s, 0), v)
        self.base[k] = b

    def op(self, eng, fn, reads=(), writes=(), tick=True, cost=0.5):
        self._check_pending(reads, writes)
        deps = self._deps(reads, writes)
        name = "E_" + eng
        if eng == "pe":
            deps.pop(name, None)
        self.cnt[name] += 1
        tok = (name, self.cnt[name])
        self.streams[eng].append((deps, fn, (name, 1)))
        self._commit(tok, reads, writes)
        if eng == "pe" and tick and self.pending:
            for p in self.pending:
                p[0] -= cost
            while self.pending and self.pending[0][0] <= 0:
                self.flush(0)

    def dma(self, queue, sem, fn, reads=(), writes=()):
        self._check_pending(reads, writes)
        self._sem(sem)
        self.is_dma_sem.add(sem)
        deps = self._deps(reads, writes)
        self.cnt[sem] += 16
        tok = (sem, self.cnt[sem])
        self.streams[queue].append((deps, fn, (sem, 16)))
        self._commit(tok, reads, writes)

    def final_wait(self, queue, sem):
        self.streams[queue].append(({sem: self.cnt[sem]}, None, None))

    def emit(self):
        nc = self.nc
        with nc.Block() as block:
            def run(ename):
                def body(e):
                    waited = {}
                    for deps, fn, inc in self.streams[ename]:
                        for s, v in deps.items():
                            if waited.get(s, 0) >= v:
                                continue
                            e.wait_ge(self.sems[s], v)
                            waited[s] = v
                        if fn is None:
                            continue
                        ins = fn(e)
                        ins.then_inc(self.sems[inc[0]], inc[1])
                return body
            if self.streams["pe"]:
                block.tensor(run("pe"))
            if self.streams["act"]:
                block.scalar(run("act"))
            if self.streams["dve"]:
                block.vector(run("dve"))
            if self.streams["pool"]:
                block.gpsimd(run("pool"))
            if self.streams["sp"]:
                block.sync(run("sp"))


class SG:
    def __init__(self, name, ptiles, has_sample, groups):
        self.name = name
        self.ptiles = ptiles
        self.NP = len(ptiles)
        self.has_sample = has_sample
        self.NT = self.NP + (1 if has_sample else 0)
        self.TOK = self.NT * 128
        self.PL = self.NP * 128
        self.groups = groups
        assert sum(g[1] for g in groups) == self.TOK


def build_nc():
    nc = bass.Bass("TRN2", target_bir_lowering=False)
    stack = ExitStack()

    def din(name, shape):
        return nc.dram_tensor(name, list(shape), F32, kind="ExternalInput").ap()

    def dout(name, shape):
        return nc.dram_tensor(name, list(shape), F32, kind="ExternalOutput").ap()

    xp = din("xp", (2048, D))
    xs = din("xs", (128, D))
    cache = din("cache", (2, 32, D))
    ln_g = din("ln_g", (DEPTH, 3, D))
    ln_b = din("ln_b", (DEPTH, 3, D))
    f1_in = din("ffn1_w_in", (DEPTH, D, 2 * DFF))
    f1_out = din("ffn1_w_out", (DEPTH, DFF, D))
    f2_in = din("ffn2_w_in", (DEPTH, D, 2 * DFF))
    f2_out = din("ffn2_w_out", (DEPTH, DFF, D))
    cv_in = din("conv_w_in", (2, D, 3 * D))
    cv_w = din("conv_w", (2, 3, D))
    cv_out = din("conv_w_out", (2, D, D))
    sg_in = din("sgu_w_in", (2, D, 2 * DSGU))
    sg_g = din("sgu_ln_g", (2, DSGU))
    sg_b = din("sgu_ln_b", (2, DSGU))
    sg_ws = din("sgu_w_s", (2, 8, 128, 128))
    sg_bs = din("sgu_b_s", (2, 8, 128))
    sg_out = din("sgu_w_out", (2, DSGU, D))
    c_ident = din("c_ident", (128, 128))
    c_tril = din("c_tril", (128, 128))
    c_blk = din("c_blk", (128, 128))

    y_p = dout("y_p", (2048, D))
    y_s = dout("y_s", (128, D))
    st_cp = dout("st_cp", (2, 2, D))
    st_cs = dout("st_cs", (2, 32, D))
    st_sv = dout("st_sv", (2, 128, DSGU))

    def sb(name, shape, dt):
        return stack.enter_context(nc.sbuf_tensor(name, list(shape), dt))

    xres = sb("xres", (128, MAXNT, D), F32)
    xT = sb("xT", (128, 8, MAXTOK), BF16)
    BIG_F32 = 27712
    big = sb("big", (128, BIG_F32), F32)
    wring = sb("wring", (128, 8192), BF16)
    lnp = sb("lnp", (128, 2, D), F32)
    tbuf = sb("tbuf", (128, 2, D), F32)
    xbf = sb("xbf", (128, 2, D), BF16)
    sgt = sb("sgt", (128, 2, 512), F32)
    stt = sb("stt", (128, 2, 40), F32)
    ident_f = sb("ident_f", (128, 128), F32)
    ident_b = sb("ident_b", (128, 128), BF16)
    tril_f = sb("tril_f", (128, 128), F32)
    blk_f = sb("blk_f", (128, 128), F32)
    ones_b = sb("ones_b", (128, 128), BF16)
    eps_t = sb("eps_t", (128, 8), F32)
    colp = sb("colp", (128, 144), F32)
    pnat = sb("pnat", (128, 2, 128), F32)
    hist = sb("hist", (128, 2, 8, 2), F32)
    ps = [stack.enter_context(nc.psum_tensor("ps%d" % i, [128, 512], F32)) for i in range(8)]

    P = Prog(nc, stack)
    PS7 = [("ps7m", 0), ("ps7m", 1)]

    def pk(q):
        return list(PS7) if q == 7 else ["ps%d" % q]

    def bigv(off_f32, n_f32, dt=F32):
        v = big[:, off_f32:off_f32 + n_f32]
        if dt == BF16:
            v = v.bitcast(BF16)
        return v

    P.dma("sp", "c0", lambda e: e.dma_start(out=ident_f[:], in_=c_ident[:, :]), writes=["ident_f"])
    P.dma("sp", "c1", lambda e: e.dma_start(out=tril_f[:], in_=c_tril[:, :]), writes=["tril_f"])
    P.dma("sp", "c2", lambda e: e.dma_start(out=blk_f[:], in_=c_blk[:, :]), writes=["blk_f"])
    P.op("act", lambda e: e.copy(out=ident_b[:], in_=ident_f[:]), reads=["ident_f"], writes=["ident_b"])
    P.op("dve", lambda e: e.memset(ones_b[:], 1.0), writes=["ones_b"])
    P.op("dve", lambda e: e.memset(eps_t[:], EPS), writes=["eps_t"])
    P.dma("sp", "c3", lambda e: e.dma_start(
        out=pnat[0:48, 0, :], in_=cv_w.rearrange("j k (c p) -> (j k c) p", p=128)), writes=["pnat0"])
    P.dma("sp", "c3", lambda e: e.dma_start(
        out=pnat[48:96, 0, :], in_=sg_g.rearrange("j (c p) -> (j c) p", p=128)), writes=["pnat1"])
    P.dma("sp", "c3", lambda e: e.dma_start(
        out=pnat[0:48, 1, :], in_=sg_b.rearrange("j (c p) -> (j c) p", p=128)), writes=["pnat2"])

    def _tp(e):
        e.transpose(out=ps[6][:, 0:96], in_=pnat[0:96, 0, :], identity=ident_f[0:96, 0:96])
        return e.transpose(out=ps[6][:, 96:144], in_=pnat[0:48, 1, :], identity=ident_f[0:48, 0:48])
    P.op("pe", _tp, reads=["pnat0", "pnat1", "pnat2", "ident_f"], writes=["ps6"])
    P.op("act", lambda e: e.copy(out=colp[:, :], in_=ps[6][:, 0:144]), reads=["ps6"], writes=["colp"])

    def cw_col(j, k, c):
        i = (j * 3 + k) * 8 + c
        return colp[:, i:i + 1]

    def sgg_col(j, c):
        i = 48 + j * 24 + c
        return colp[:, i:i + 1]

    def sgb_col(j, c):
        i = 96 + j * 24 + c
        return colp[:, i:i + 1]

    xbf_ctr = [0]
    t_ctr = [0]
    sg_ctr = [0]

    def emit_xT(i, defer=True):
        b = xbf_ctr[0] % 2
        xbf_ctr[0] += 1
        P.op("act", lambda e: e.copy(out=xbf[:, b, :], in_=xres[:, i, :]),
             reads=[("xres", i)], writes=[("xbf", b)])
        tbank = 6 if b == 0 else 7
        tb = ps[tbank][:, :].bitcast(BF16)

        def _t(e):
            ins = None
            for k in range(8):
                ins = e.transpose(out=tb[:, k * 128:(k + 1) * 128], in_=xbf[:, b, k * 128:(k + 1) * 128],
                                  identity=ident_b[:, :])
            return ins

        def emit_tail():
            P.op("pe", _t, reads=[("xbf", b), "ident_b"], writes=pk(tbank), tick=False)
            P.op("act", lambda e: e.copy(out=xT[:, :, i * 128:(i + 1) * 128],
                                         in_=tb.rearrange("p (k t) -> p k t", k=8)),
                 reads=pk(tbank), writes=[("xT", i)])
        if defer:
            P.defer(DEFER_N, [("xT", i), ("xbf", b)] + pk(tbank), emit_tail)
        else:
            emit_tail()

    def load_lnp(l, s):
        P.dma("sp", "lnp", lambda e: e.dma_start(out=lnp[:, 0, :], in_=ln_g[l, s, :].partition_broadcast(128)),
              writes=["lnp0"])
        P.dma("sp", "lnp", lambda e: e.dma_start(out=lnp[:, 1, :], in_=ln_b[l, s, :].partition_broadcast(128)),
              writes=["lnp1"])

    def out_rows(sgd, i):
        if i < sgd.NP:
            g = sgd.ptiles[i]
            return y_p[g * 128:(g + 1) * 128, :]
        return y_s[:, :]

    epi_pending = []

    def flush_epi():
        while epi_pending:
            epi_pending.pop(0)()

    def epilogue(sgd, i, y0, y1, x_scale, is_last, in_xres=None):
        b = t_ctr[0] % 2
        t_ctr[0] += 1
        st = stt[:, b, :]
        for h, yb in enumerate((y0, y1)):
            P.op("dve", lambda e, h=h, yb=yb: e.scalar_tensor_tensor(
                out=tbuf[:, b, h * 512:(h + 1) * 512], in0=xres[:, i, h * 512:(h + 1) * 512], scalar=float(x_scale),
                in1=ps[yb][:, :], op0=ALU.mult, op1=ALU.add),
                reads=[("xres", i), "ps%d" % yb], writes=[("t", b, h)])
        for h in range(2):
            P.op("dve", lambda e, h=h: e.bn_stats(out=st[:, h * 6:(h + 1) * 6], in_=tbuf[:, b, h * 512:(h + 1) * 512]),
                 reads=[("t", b, h)], writes=[("st", b, h)])
        P.op("dve", lambda e: e.bn_aggr(out=st[:, 12:14], in_=st[:, 0:12].rearrange("p (n t) -> p n t", t=3)),
             reads=[("st", b, 0), ("st", b, 1)], writes=[("mv", b)])
        P.op("act", lambda e: e.activation(out=st[:, 16:17], in_=st[:, 13:14], func=AF.Ln, bias=eps_t[:, 0:1], scale=1.0),
             reads=[("mv", b), "eps_t"], writes=[("lnv", b)])
        P.op("act", lambda e: e.activation(out=st[:, 14:15], in_=st[:, 16:17], func=AF.Exp, scale=-0.5),
             reads=[("lnv", b)], writes=[("rstd", b)])
        P.op("dve", lambda e: e.tensor_scalar(out=st[:, 17:18], in0=st[:, 12:13], scalar1=-1.0, scalar2=None,
                                              op0=ALU.mult),
             reads=[("mv", b)], writes=[("negm", b)])
        P.op("act", lambda e: e.activation(out=st[:, 15:16], in_=st[:, 14:15], func=AF.Identity, scale=st[:, 17:18]),
             reads=[("rstd", b), ("negm", b)], writes=[("nmr", b)])
        P.op("act", lambda e: e.activation(out=tbuf[:, b, :], in_=tbuf[:, b, :], func=AF.Identity,
                                           bias=st[:, 15:16], scale=st[:, 14:15]),
             reads=[("t", b, 0), ("t", b, 1), ("rstd", b), ("nmr", b)], writes=[("t", b, 0), ("t", b, 1)])

        def stage_b():
            P.op("dve", lambda e: e.tensor_tensor(out=tbuf[:, b, :], in0=tbuf[:, b, :], in1=lnp[:, 0, :], op=ALU.mult),
                 reads=[("t", b, 0), ("t", b, 1), "lnp0", "lnp1"], writes=[("t", b, 0), ("t", b, 1), "lnp0", "lnp1"])
            P.op("dve", lambda e: e.tensor_tensor(out=xres[:, i, :], in0=tbuf[:, b, :], in1=lnp[:, 1, :], op=ALU.add),
                 reads=[("t", b, 0), ("t", b, 1), "lnp0", "lnp1"], writes=[("xres", i), "lnp0", "lnp1"])
            if is_last:
                P.dma("sp", "yo%d" % i, lambda e: e.dma_start(out=out_rows(sgd, i), in_=xres[:, i, :]),
                      reads=[("xres", i)])
            else:
                emit_xT(i)
        flush_epi()
        epi_pending.append(stage_b)

    def out_proj(sgd, i, n_chunks, lhs_fn, rhs_fn, key_reads, key_writes=()):
        for n, yb in enumerate((4, 5)):
            def _m(e, n=n, yb=yb):
                ins = None
                for c in range(n_chunks):
                    ins = e.matmul(ps[yb][:, :], lhsT=lhs_fn(c, i), rhs=rhs_fn(c)[:, n * 512:(n + 1) * 512],
                                   start=(c == 0), stop=(c == n_chunks - 1))
                return ins
            P.op("pe", _m, reads=key_reads, writes=["ps%d" % yb] + list(key_writes), cost=n_chunks * 512 / 1950.0)

    def ffn(sgd, l, which, is_last):
        w_in = (f1_in, f2_in)[which][l].rearrange("(k p) n -> p k n", p=128)
        w_out = (f1_out, f2_out)[which][l].rearrange("(j p) n -> p j n", p=128)
        TOK = sgd.TOK
        hT = bigv(0, NJ * MAXTOK // 2, BF16).rearrange("p (j t) -> p j t", j=NJ)
        wo_off = NJ * MAXTOK // 2
        wout = bigv(wo_off, NJ * D // 2, BF16).rearrange("p (j n) -> p j n", j=NJ)
        assert wo_off + NJ * D // 2 <= BIG_F32
        ws4 = wring[:, :].rearrange("p (s k n) -> p s k n", s=4, k=8)
        sp_off = wo_off + NJ * D // 2
        ws3 = bigv(sp_off, 3 * 1024, BF16).rearrange("p (s k n) -> p s k n", s=3, k=8)
        assert sp_off + 3 * 1024 <= BIG_F32
        NS = 7
        JB = 4 if len(sgd.groups) == 2 else 3

        def wsl(sl):
            return ws4[:, sl] if sl < 4 else ws3[:, sl - 4]
        wo_piece = [0]

        def issue_wout():
            j0 = wo_piece[0]
            if j0 >= NJ:
                return
            j1 = min(NJ, j0 + 2)
            wo_piece[0] = j1
            P.dma("pool", "wout", lambda e: e.dma_start(out=wout[:, j0:j1, :], in_=w_out[:, j0:j1, :]),
                  reads=["BIGL"], writes=[("wout", j) for j in range(j0, j1)])

        for j0 in range(0, NJ, JB):
            blk = list(range(j0, min(NJ, j0 + JB)))
            for j in blk:
                sl = j % NS
                P.dma("pool", "wr%d" % sl, lambda e, sl=sl, j=j: e.dma_start(
                    out=wsl(sl)[:, :, 0:128], in_=w_in[:, :, j * 128:(j + 1) * 128]),
                    reads=["WRL", "BIGL"], writes=[("wr", sl, 0)])
                P.dma("pool", "wr%d" % sl, lambda e, sl=sl, j=j: e.dma_start(
                    out=wsl(sl)[:, :, 128:256], in_=w_in[:, :, DFF + j * 128:DFF + (j + 1) * 128]),
                    reads=["WRL", "BIGL"], writes=[("wr", sl, 1)])
                if j >= NS - 1:
                    issue_wout()
            for (g0, gs) in sgd.groups:
                for j in blk:
                    sl = j % NS
                    b = sg_ctr[0] % 2
                    sg_ctr[0] += 1
                    gb, ub = (0, 2) if b == 0 else (1, 3)

                    def _m(e, sl=sl, g0=g0, gs=gs, gb=gb, ub=ub):
                        ins = None
                        for half, bank in ((0, gb), (1, ub)):
                            for k in range(8):
                                ins = e.matmul(ps[bank][:, 0:gs], lhsT=wsl(sl)[:, k, half * 128:(half + 1) * 128],
                                               rhs=xT[:, k, g0:g0 + gs], start=(k == 0), stop=(k == 7))
                        return ins
                    tiles = range(g0 // 128, (g0 + gs) // 128)
                    P.op("pe", _m, reads=["WRL", "BIGL", ("wr", sl, 0), ("wr", sl, 1)] + [("xT", i) for i in tiles],
                         writes=["ps%d" % gb, "ps%d" % ub, ("wr", sl, 0), ("wr", sl, 1)], cost=16 * gs / 1950.0)
                    P.op("act", lambda e, b=b, gb=gb, gs=gs: e.activation(out=sgt[:, b, 0:gs], in_=ps[gb][:, 0:gs],
                                                                           func=AF.Silu),
                         reads=["ps%d" % gb], writes=[("sgt", b)])
                    P.op("dve", lambda e, b=b, ub=ub, j=j, g0=g0, gs=gs: e.scalar_tensor_tensor(
                        out=hT[:, j, g0:g0 + gs], in0=sgt[:, b, 0:gs], scalar=0.5, in1=ps[ub][:, 0:gs],
                        op0=ALU.mult, op1=ALU.mult),
                        reads=[("sgt", b), "ps%d" % ub, "BIGL"], writes=[("hT", j)])
        while wo_piece[0] < NJ:
            issue_wout()
        load_lnp(l, 0 if which == 0 else 2)
        for i in range(sgd.NT):
            out_proj(sgd, i, NJ, lambda c, i: hT[:, c, i * 128:(i + 1) * 128], lambda c: wout[:, c, :],
                     [("hT", j) for j in range(NJ)] + [("wout", j) for j in range(NJ)] + ["BIGL"],
                     [("wout", j) for j in range(NJ)])
            epilogue(sgd, i, 4, 5, ALPHA, is_last)
        flush_epi()

    def conv(sgd, l):
        j = l // 2
        w_in = cv_in[j].rearrange("(k p) n -> p k n", p=128)
        w_out = cv_out[j].rearrange("(c p) n -> p c n", p=128)
        TOK, PL = sgd.TOK, sgd.PL
        off = 0
        yT = bigv(off, 8 * MAXTOK // 2, BF16).rearrange("p (c t) -> p c t", c=8); off += 8 * MAXTOK // 2
        wout = bigv(off, 8 * D // 2, BF16).rearrange("p (c n) -> p c n", c=8); off += 8 * D // 2
        gbuf = bigv(off, 2 * 1032).rearrange("p (b t) -> p b t", b=2); off += 2 * 1032
        gsb = bigv(off, 2 * 160).rearrange("p (b a t) -> p b a t", b=2, a=16); off += 2 * 160
        bT = bigv(off, 2 * MAXTOK).rearrange("p (b t) -> p b t", b=2); off += 2 * MAXTOK
        cvb = bigv(off, MAXTOK); off += MAXTOK
        stT = bigv(off, 8 * 34).rearrange("p (c r) -> p c r", c=8); off += 8 * 34
        cacheT = bigv(off, 8 * 32).rearrange("p (c r) -> p c r", c=8); off += 8 * 32
        cnat = bigv(off, D); off += D
        stout = bigv(off, D); off += D
        assert off <= BIG_F32
        wslots = wring[:, 0:2 * 8 * 384].rearrange("p (s k n) -> p s k n", s=2, k=8)
        load_lnp(l, 1)
        def issue_cwout():
            P.dma("pool", "wout", lambda e: e.dma_start(out=wout[:, :, :], in_=w_out[:, :, :]),
                  reads=["BIGL"], writes=["cwout"])
        if sgd.has_sample:
            P.dma("sp", "cnat", lambda e: e.dma_start(out=cnat[0:32, :], in_=cache[j, :, :]),
                  reads=["BIGL"], writes=["cnat"])

            def _tc(e):
                ins = None
                for c in range(8):
                    ins = e.transpose(out=ps[7][:, c * 32:(c + 1) * 32], in_=cnat[0:32, c * 128:(c + 1) * 128],
                                      identity=ident_f[0:32, 0:32])
                return ins
            P.op("pe", _tc, reads=["cnat", "ident_f", "BIGL"], writes=PS7)
            P.op("act", lambda e: e.copy(out=cacheT[:, :, :], in_=ps[7][:, 0:256].rearrange("p (c r) -> p c r", c=8)),
                 reads=PS7 + ["BIGL"], writes=["cacheT"])
        for c in range(8):
            s = c % 2
            cb = c % 2
            for part in range(3):
                P.dma("pool", "wr%d" % s, lambda e, s=s, c=c, part=part: e.dma_start(
                    out=wslots[:, s, :, part * 128:(part + 1) * 128],
                    in_=w_in[:, :, part * D + c * 128:part * D + (c + 1) * 128]), reads=["WRL"], writes=[("wr", s, part)])
            if c == 1:
                issue_cwout()
            for (g0, gs) in sgd.groups:
                b = sg_ctr[0] % 2
                sg_ctr[0] += 1
                banks = (0, 1, 2) if b == 0 else (3, 7, 6)

                def _m(e, s=s, g0=g0, gs=gs, banks=banks):
                    ins = None
                    for part in range(3):
                        for k in range(8):
                            ins = e.matmul(ps[banks[part]][:, 0:gs], lhsT=wslots[:, s, k, part * 128:(part + 1) * 128],
                                           rhs=xT[:, k, g0:g0 + gs], start=(k == 0), stop=(k == 7))
                    return ins
                tiles = range(g0 // 128, (g0 + gs) // 128)
                P.op("pe", _m, reads=["WRL", ("wr", s, 0), ("wr", s, 1), ("wr", s, 2)] + [("xT", i) for i in tiles],
                     writes=sum([pk(q) for q in banks], []) + [("wr", s, 0), ("wr", s, 1), ("wr", s, 2)],
                     cost=24 * gs / 1950.0)
                bb, cbk, hb = banks
                P.op("act", lambda e, b=b, cbk=cbk, gs=gs: e.copy(out=sgt[:, b, 0:gs], in_=ps[cbk][:, 0:gs]),
                     reads=pk(cbk), writes=[("sgt", b)])
                p1 = min(g0 + gs, PL)
                if g0 < p1:
                    n = p1 - g0
                    P.op("dve", lambda e, b=b, hb=hb, cb=cb, g0=g0, n=n: e.tensor_tensor(
                        out=gbuf[:, cb, 2 + g0:2 + g0 + n], in0=sgt[:, b, 0:n], in1=ps[hb][:, 0:n], op=ALU.mult),
                        reads=[("sgt", b), "BIGL"] + pk(hb), writes=[("gbuf", cb)])
                if g0 + gs > PL:
                    o = PL - g0
                    P.op("dve", lambda e, b=b, hb=hb, cb=cb, o=o: e.tensor_tensor(
                        out=gsb[:, cb, :, 2:10], in0=sgt[:, b, o:o + 128].rearrange("p (a t) -> p a t", a=16),
                        in1=ps[hb][:, o:o + 128].rearrange("p (a t) -> p a t", a=16), op=ALU.mult),
                        reads=[("sgt", b), "BIGL"] + pk(hb), writes=[("gsb", cb)])
                P.op("act", lambda e, bb=bb, cb=cb, g0=g0, gs=gs: e.copy(out=bT[:, cb, g0:g0 + gs], in_=ps[bb][:, 0:gs]),
                     reads=pk(bb) + ["BIGL"], writes=[("bT", cb)])
            if sgd.name == "A":
                P.op("dve", lambda e, cb=cb: e.memset(gbuf[:, cb, 0:2], 0.0), reads=["BIGL"], writes=[("gbuf", cb)])
            else:
                P.op("dve", lambda e, cb=cb, c=c: e.tensor_copy(out=gbuf[:, cb, 0:2], in_=hist[:, j, c, :]),
                     reads=[("hist", j, c), "BIGL"], writes=[("gbuf", cb)])
            if sgd.has_sample:
                P.op("dve", lambda e, cb=cb, c=c: e.tensor_copy(
                    out=gsb[:, cb, :, 0:2], in_=cacheT[:, c, :].rearrange("p (a r) -> p a r", a=16)),
                    reads=["cacheT", "BIGL"], writes=[("gsb", cb)])
            rk = [("gbuf", cb), ("gsb", cb), "colp", "BIGL"]
            P.op("dve", lambda e, cb=cb, c=c: e.tensor_scalar(out=cvb[:, 0:PL], in0=gbuf[:, cb, 0:PL],
                                                             scalar1=cw_col(j, 0, c), scalar2=None, op0=ALU.mult),
                 reads=rk, writes=["cvb"])
            for k in (1, 2):
                P.op("dve", lambda e, cb=cb, c=c, k=k: e.scalar_tensor_tensor(
                    out=cvb[:, 0:PL], in0=gbuf[:, cb, k:k + PL], scalar=cw_col(j, k, c), in1=cvb[:, 0:PL],
                    op0=ALU.mult, op1=ALU.add), reads=rk + ["cvb"], writes=["cvb"])
            if sgd.has_sample:
                cvs = cvb[:, PL:PL + 128].rearrange("p (a t) -> p a t", a=16)
                P.op("dve", lambda e, cb=cb, c=c: e.tensor_scalar(out=cvs, in0=gsb[:, cb, :, 0:8],
                                                                 scalar1=cw_col(j, 0, c), scalar2=None, op0=ALU.mult),
                     reads=rk, writes=["cvs"])
                for k in (1, 2):
                    P.op("dve", lambda e, cb=cb, c=c, k=k: e.scalar_tensor_tensor(
                        out=cvs, in0=gsb[:, cb, :, k:k + 8], scalar=cw_col(j, k, c), in1=cvs,
                        op0=ALU.mult, op1=ALU.add), reads=rk + ["cvs"], writes=["cvs"])
            P.op("dve", lambda e, cb=cb, c=c: e.tensor_tensor(out=yT[:, c, 0:TOK], in0=cvb[:, 0:TOK],
                                                             in1=bT[:, cb, 0:TOK], op=ALU.mult),
                 reads=["cvb", "cvs", ("bT", cb), "BIGL"], writes=[("yT", c)])
            if sgd.name == "A":
                P.op("dve", lambda e, cb=cb, c=c: e.tensor_copy(out=hist[:, j, c, :], in_=gbuf[:, cb, PL:PL + 2]),
                     reads=[("gbuf", cb), "BIGL"], writes=[("hist", j, c)])
            else:
                P.op("dve", lambda e, cb=cb, c=c: e.tensor_copy(
                    out=stT[:, c, 0:32].rearrange("p (a r) -> p a r", a=16), in_=gsb[:, cb, :, 8:10]),
                    reads=[("gsb", cb), "BIGL"], writes=[("stT", c)])
                P.op("dve", lambda e, cb=cb, c=c: e.tensor_copy(out=stT[:, c, 32:34], in_=gbuf[:, cb, PL:PL + 2]),
                     reads=[("gbuf", cb), "BIGL"], writes=[("stT", c)])
        if sgd.name == "B":
            for half, bank in ((0, 0), (1, 1)):
                def _ts(e, half=half, bank=bank):
                    ins = None
                    for q in range(4):
                        c = half * 4 + q
                        ins = e.transpose(out=ps[bank][0:34, q * 128:(q + 1) * 128], in_=stT[:, c, :],
                                          identity=ident_f[:, :])
                    return ins
                P.op("pe", _ts, reads=[("stT", c) for c in range(8)] + ["ident_f", "BIGL"], writes=["ps%d" % bank])
                P.op("act", lambda e, half=half, bank=bank: e.copy(out=stout[0:34, half * 512:(half + 1) * 512],
                                                                   in_=ps[bank][0:34, :]),
                     reads=["ps%d" % bank, "BIGL"], writes=[("stout", half)])
            P.dma("sp", "sto", lambda e: e.dma_start(out=st_cs[j, :, :], in_=stout[0:32, :]),
                  reads=[("stout", 0), ("stout", 1), "BIGL"])
            P.dma("sp", "sto", lambda e: e.dma_start(out=st_cp[j, :, :], in_=stout[32:34, :]),
                  reads=[("stout", 0), ("stout", 1), "BIGL"])
        for i in range(sgd.NT):
            out_proj(sgd, i, 8, lambda c, i: yT[:, c, i * 128:(i + 1) * 128], lambda c: wout[:, c, :],
                     [("yT", c) for c in range(8)] + ["cwout", "BIGL"])
            epilogue(sgd, i, 4, 5, ALPHA, False)
        flush_epi()

    def sgu(sgd, l):
        j = l // 2
        w_in = sg_in[j].rearrange("(k p) n -> p k n", p=128)
        w_out = sg_out[j].rearrange("(c p) n -> p c n", p=128)
        TOK, PL, NT = sgd.TOK, sgd.PL, sgd.NT
        off = 0
        vraw = bigv(off, MAXNT * DSGU // 2, BF16).rearrange("p (i f) -> p i f", i=MAXNT); off += MAXNT * DSGU // 2
        uT = bigv(off, 6 * MAXTOK // 2, BF16).rearrange("p (c t) -> p c t", c=6); off += 6 * MAXTOK // 2
        wout = bigv(off, 6 * D // 2, BF16).rearrange("p (c n) -> p c n", c=6); off += 6 * D // 2
        WTp = bigv(off, 8 * 128 // 2, BF16).rearrange("p (h t) -> p h t", h=8); off += 8 * 128 // 2
        WTs = bigv(off, 8 * 128 // 2, BF16).rearrange("p (h t) -> p h t", h=8); off += 8 * 128 // 2
        bbp = bigv(off, 8 * 128).rearrange("p (h t) -> p h t", h=8); off += 8 * 128
        B2p = bigv(off, 6 * 128).rearrange("p (c t) -> p c t", c=6); off += 6 * 128
        B2s = bigv(off, 6 * 128).rearrange("p (c t) -> p c t", c=6); off += 6 * 128
        wnb = bigv(off, 8 * 128 // 2, BF16).rearrange("p (h t) -> p h t", h=8)
        vn = bigv(off, 2 * 768 // 2, BF16).rearrange("p (b f) -> p b f", b=2); off += 2 * 768 // 2
        wnat = bigv(off, 8 * 128).rearrange("p (h t) -> p h t", h=8)
        mtmp = bigv(off, 2 * 768).rearrange("p (b c t) -> p b c t", b=2, c=6); off += 2 * 768
        svst = bigv(off, MAXNT * 48).rearrange("p (i q) -> p i q", i=MAXNT); off += MAXNT * 48
        assert off <= BIG_F32, off

        def prep_head(mask, sample):
            if not sample:
                P.dma("sp", "wnat", lambda e: e.dma_start(out=wnat[:, :, :], in_=sg_ws[j].rearrange("h t s -> t h s")),
                      reads=["BIGL", "PREPL"], writes=["wnat"] + [("wnatd", a) for a in range(16)])
            else:
                P.op("dve", lambda e: e.memset(wnat[:, :, :], 0.0), reads=["BIGL", "PREPL"],
                     writes=["wnat"] + [("wnatd", a) for a in range(16)])
                for a in range(16):
                    P.dma("sp", "wnat", lambda e, a=a: e.dma_start(
                        out=wnat[a * 8:(a + 1) * 8, :, a * 8:(a + 1) * 8],
                        in_=sg_ws[j, :, 0:8, 0:8].rearrange("h t s -> t h s")), reads=["PREPL", "BIGL", "wnat"],
                        writes=[("wnatd", a)])
            P.op("dve", lambda e: e.tensor_tensor(out=wnb[:, :, :], in0=wnat[:, :, :],
                                                  in1=mask[:, :].unsqueeze(1).to_broadcast([128, 8, 128]), op=ALU.mult),
                 reads=["wnat", "tril_f", "blk_f", "BIGL", "PREPL"] + [("wnatd", a) for a in range(16)],
                 writes=["wnb", "wnat"] + [("wnatd", a) for a in range(16)])

        def prep_tail(dst, sample):
            tb = ps[7][:, :].bitcast(BF16)

            def _t(e):
                ins = None
                for h in range(8):
                    ins = e.transpose(out=tb[:, h * 128:(h + 1) * 128], in_=wnb[:, h, :], identity=ident_b[:, :])
                return ins
            P.op("pe", _t, reads=["wnb", "ident_b", "PREPL", "BIGL"], writes=PS7 + ["wnb"])
            P.op("act", lambda e: e.copy(out=dst[:, :, :], in_=tb.rearrange("p (h t) -> p h t", h=8)),
                 reads=PS7 + ["BIGL"], writes=[("WT", sample)])

        prep_head(tril_f, False)
        P.dma("sp", "bb", lambda e: e.dma_start(
            out=bbp[:, :, :].rearrange("p h t -> p (h t)"),
            in_=sg_bs[j].rearrange("h t -> (h t)").partition_broadcast(128)), reads=["BIGL"], writes=["bbp"])

        vslots = wring[:, :].rearrange("p (s k n) -> p s k n", s=2, k=8)
        for f in range(6):
            s = f % 2
            P.dma("pool", "wr%d" % s, lambda e, s=s, f=f: e.dma_start(
                out=vslots[:, s, :, :], in_=w_in[:, :, DSGU + f * 512:DSGU + (f + 1) * 512]), reads=["WRL"], writes=[("wr", s, 0)])
            for i in range(NT):
                b = sg_ctr[0] % 2
                sg_ctr[0] += 1
                vb = b

                def _m(e, s=s, i=i, vb=vb):
                    ins = None
                    for k in range(8):
                        ins = e.matmul(ps[vb][:, :], lhsT=xT[:, k, i * 128:(i + 1) * 128], rhs=vslots[:, s, k, :],
                                       start=(k == 0), stop=(k == 7))
                    return ins
                P.op("pe", _m, reads=["WRL", ("wr", s, 0), ("xT", i)], writes=["ps%d" % vb], cost=8 * 512 / 1950.0)
                P.op("act", lambda e, b=b, vb=vb: e.activation(out=sgt[:, b, :], in_=ps[vb][:, :], func=AF.Gelu),
                     reads=["ps%d" % vb], writes=[("sgt", b)])
                P.op("dve", lambda e, b=b, i=i, f=f: e.bn_stats(out=svst[:, i, f * 6:(f + 1) * 6], in_=sgt[:, b, :]),
                     reads=[("sgt", b), "BIGL"], writes=[("svst", i, f)])
                P.op("act", lambda e, b=b, i=i, f=f: e.copy(out=vraw[:, i, f * 512:(f + 1) * 512], in_=sgt[:, b, :]),
                     reads=[("sgt", b), "BIGL"], writes=[("vraw", i)])
            if f == 0:
                prep_tail(WTp, False)
                if sgd.has_sample:
                    prep_head(blk_f, True)
            if f == 2 and sgd.has_sample:
                prep_tail(WTs, True)
        for i in range(NT):
            P.op("dve", lambda e, i=i: e.bn_aggr(out=svst[:, i, 36:38],
                                                 in_=svst[:, i, 0:36].rearrange("p (n t) -> p n t", t=3)),
                 reads=[("svst", i, f) for f in range(6)] + ["BIGL"], writes=[("smv", i)])
            P.op("act", lambda e, i=i: e.activation(out=svst[:, i, 40:41], in_=svst[:, i, 37:38], func=AF.Ln,
                                                    bias=eps_t[:, 0:1], scale=1.0),
                 reads=[("smv", i), "eps_t", "BIGL"], writes=[("slnv", i)])
            P.op("act", lambda e, i=i: e.activation(out=svst[:, i, 38:39], in_=svst[:, i, 40:41], func=AF.Exp, scale=-0.5),
                 reads=[("slnv", i), "BIGL"], writes=[("srstd", i)])
            P.op("dve", lambda e, i=i: e.scalar_tensor_tensor(out=svst[:, i, 39:40], in0=svst[:, i, 36:37], scalar=-1.0,
                                                              in1=svst[:, i, 38:39], op0=ALU.mult, op1=ALU.mult),
                 reads=[("smv", i), ("srstd", i), "BIGL"], writes=[("snmr", i)])

        load_lnp(l, 1)
        P.fence("PREPL")

        P.fence("WRL")
        uslots = wring[:, :].rearrange("p (s k n) -> p s k n", s=8, k=8)
        m_ctr = [0]

        def issue_wu(part):
            for cl in range(6):
                cg = part * 6 + cl
                s = cg % 8
                P.dma("pool", "wu%d" % s, lambda e, s=s, cg=cg: e.dma_start(
                    out=uslots[:, s, :, :], in_=w_in[:, :, cg * 128:(cg + 1) * 128]), reads=["WRL"], writes=[("wu", s)])
        for part in range(4):
            if sgd.has_sample and part < 3:
                for q in range(2):
                    f = 2 * part + q
                    P.dma("sp", "gb%d" % q, lambda e, q=q, f=f: e.dma_start(
                        out=tbuf[:, 0, q * 512:(q + 1) * 512], in_=sg_g[j, f * 512:(f + 1) * 512].partition_broadcast(128)),
                        writes=[("t", 0, q)])
                    P.dma("sp", "gb%d" % (2 + q), lambda e, q=q, f=f: e.dma_start(
                        out=tbuf[:, 1, q * 512:(q + 1) * 512], in_=sg_b[j, f * 512:(f + 1) * 512].partition_broadcast(128)),
                        writes=[("t", 1, q)])
            variants = [(0, WTp, B2p)] + ([(1, WTs, B2s)] if sgd.has_sample else [])
            for variant, WT, B2 in variants:
                for hh in range(2):
                    h = part * 2 + hh
                    r0 = (variant * 2 + hh) * 128
                    P.op("pe", lambda e, WT=WT, h=h, r0=r0: e.matmul(ps[7][:, r0:r0 + 128], lhsT=ones_b[:, :],
                                                                    rhs=WT[:, h, :], start=True, stop=True),
                         reads=[("WT", variant == 1), "ones_b", "BIGL"], writes=[("ps7m", variant)])
            for variant, WT, B2 in variants:
                for hh in range(2):
                    h = part * 2 + hh
                    r0 = (variant * 2 + hh) * 128
                    for q in range(3):
                        cl = hh * 3 + q
                        if variant == 0:
                            P.op("dve", lambda e, cl=cl, h=h, B2=B2, part=part, r0=r0: e.scalar_tensor_tensor(
                                out=B2[:, cl, :], in0=ps[7][:, r0:r0 + 128], scalar=sgb_col(j, part * 6 + cl),
                                in1=bbp[:, h, :], op0=ALU.mult, op1=ALU.add),
                                reads=[("ps7m", variant), "bbp", "colp", "BIGL"], writes=[("B2", variant)])
                        else:
                            P.op("dve", lambda e, cl=cl, h=h, B2=B2, part=part, r0=r0: e.scalar_tensor_tensor(
                                out=B2[:, cl, :].rearrange("p (a t) -> p a t", a=16),
                                in0=ps[7][:, r0:r0 + 128].rearrange("p (a t) -> p a t", a=16),
                                scalar=sgb_col(j, part * 6 + cl),
                                in1=bbp[:, h, 0:8].unsqueeze(1).to_broadcast([128, 16, 8]),
                                op0=ALU.mult, op1=ALU.add),
                                reads=[("ps7m", variant), "bbp", "colp", "BIGL"], writes=[("B2", variant)])
            if part == 0:
                issue_wu(0)
            for cl in range(6):
                cg = part * 6 + cl
                s = cg % 8
                for (g0, gs) in sgd.groups:
                    b = sg_ctr[0] % 2
                    sg_ctr[0] += 1
                    ub = 2 + b

                    def _m(e, s=s, g0=g0, gs=gs, ub=ub):
                        ins = None
                        for k in range(8):
                            ins = e.matmul(ps[ub][:, 0:gs], lhsT=uslots[:, s, k, :], rhs=xT[:, k, g0:g0 + gs],
                                           start=(k == 0), stop=(k == 7))
                        return ins
                    tiles = range(g0 // 128, (g0 + gs) // 128)
                    P.op("pe", _m, reads=["WRL", ("wu", s)] + [("xT", i) for i in tiles], writes=["ps%d" % ub],
                         cost=8 * gs / 1950.0)
                    P.op("act", lambda e, ub=ub, cl=cl, g0=g0, gs=gs: e.activation(
                        out=uT[:, cl, g0:g0 + gs], in_=ps[ub][:, 0:gs], func=AF.Gelu),
                        reads=["ps%d" % ub, "BIGL"], writes=[("uT", cl)])
            P.dma("pool", "wout", lambda e, part=part: e.dma_start(out=wout[:, :, :],
                                                                  in_=w_out[:, part * 6:(part + 1) * 6, :]),
                  reads=["BIGL"], writes=["swout"])
            if part < 3:
                issue_wu(part + 1)
            def emit_vn(i):
                mb = i % 2
                P.op("act", lambda e, i=i, mb=mb, part=part: e.activation(
                    out=vn[:, mb, :], in_=vraw[:, i, part * 768:(part + 1) * 768], func=AF.Identity,
                    bias=svst[:, i, 39:40], scale=svst[:, i, 38:39]),
                    reads=[("vraw", i), ("srstd", i), ("snmr", i), "BIGL", "PREPL"], writes=[("vn", mb)])

            def mix_gate(i):
                sample = (i >= sgd.NP)
                WT = WTs if sample else WTp
                B2 = B2s if sample else B2p
                mb = i % 2

                def mview(cl, mb=mb):
                    if cl < 4:
                        return ps[0 if mb == 0 else 1][:, cl * 128:(cl + 1) * 128]
                    q = (cl - 4) + 2 * mb
                    return ps[7][:, q * 128:(q + 1) * 128]
                mkeys = ["ps%d" % (0 if mb == 0 else 1), ("ps7m", mb)]

                def _mm(e, mb=mb, part=part, WT=WT, mview=mview):
                    ins = None
                    for cl in range(6):
                        h = part * 2 + cl // 3
                        ins = e.matmul(mview(cl), lhsT=vn[:, mb, cl * 128:(cl + 1) * 128], rhs=WT[:, h, :],
                                       start=True, stop=True)
                    return ins
                P.op("pe", _mm, reads=[("vn", mb), ("WT", sample), "BIGL"], writes=mkeys)
                for cl in range(6):
                    P.op("dve", lambda e, cl=cl, mb=mb, part=part, B2=B2, mview=mview: e.scalar_tensor_tensor(
                        out=mtmp[:, mb, cl, :], in0=mview(cl), scalar=sgg_col(j, part * 6 + cl), in1=B2[:, cl, :],
                        op0=ALU.mult, op1=ALU.add),
                        reads=mkeys + [("B2", int(sample)), "colp", "BIGL", "PREPL"], writes=[("mtmp", mb, cl)])
                P.op(GATE_ENG, lambda e, i=i, mb=mb: e.tensor_tensor(
                    out=uT[:, :, i * 128:(i + 1) * 128], in0=mtmp[:, mb, :, :], in1=uT[:, :, i * 128:(i + 1) * 128],
                    op=ALU.mult),
                    reads=[("mtmp", mb, cl) for cl in range(6)] + [("uT", cl) for cl in range(6)] + ["BIGL"],
                    writes=[("uTg", i)])
            def outp(i):
                out_proj(sgd, i, 6, lambda c, i: uT[:, c, i * 128:(i + 1) * 128], lambda c: wout[:, c, :],
                         [("uTg", i), "swout", "BIGL"] + [("uT", cl) for cl in range(6)])
                if part < 3:
                    sc = ALPHA if part == 0 else 1.0
                    for h, yb in enumerate((4, 5)):
                        P.op("dve", lambda e, i=i, h=h, yb=yb, sc=sc: e.scalar_tensor_tensor(
                            out=xres[:, i, h * 512:(h + 1) * 512], in0=xres[:, i, h * 512:(h + 1) * 512],
                            scalar=float(sc), in1=ps[yb][:, :], op0=ALU.mult, op1=ALU.add),
                            reads=[("xres", i), "ps%d" % yb], writes=[("xres", i)])
                else:
                    epilogue(sgd, i, 4, 5, 1.0, False)
            emit_vn(0)
            for i in range(NT):
                if i + 1 < NT:
                    emit_vn(i + 1)
                mix_gate(i)
                if i >= 1:
                    outp(i - 1)
            outp(NT - 1)
            flush_epi()
            if sgd.has_sample and part < 3:
                i = NT - 1
                for q in range(2):
                    f = 2 * part + q
                    P.op("act", lambda e, q=q, f=f, i=i: e.activation(
                        out=sgt[:, q, :], in_=vraw[:, i, f * 512:(f + 1) * 512], func=AF.Identity,
                        bias=svst[:, i, 39:40], scale=svst[:, i, 38:39]),
                        reads=[("vraw", i), ("srstd", i), ("snmr", i), "BIGL"], writes=[("sgt", q)])
                    P.op("dve", lambda e, q=q: e.tensor_tensor(out=sgt[:, q, :], in0=sgt[:, q, :],
                                                               in1=tbuf[:, 0, q * 512:(q + 1) * 512], op=ALU.mult),
                         reads=[("sgt", q), ("t", 0, q)], writes=[("sgt", q), ("t", 0, q)])
                    P.op("dve", lambda e, q=q: e.tensor_tensor(out=sgt[:, q, :], in0=sgt[:, q, :],
                                                               in1=tbuf[:, 1, q * 512:(q + 1) * 512], op=ALU.add),
                         reads=[("sgt", q), ("t", 1, q)], writes=[("sgt", q), ("t", 1, q)])
                    P.dma("sp", "svo%d" % q, lambda e, q=q, f=f: e.dma_start(out=st_sv[j, :, f * 512:(f + 1) * 512],
                                                                       in_=sgt[:, q, :]),
                          reads=[("sgt", q)])

    sgA = SG("A", list(range(0, 8)), False, [(0, 512), (512, 512)])
    sgB = SG("B", list(range(8, 16)), True, [(0, 384), (384, 384), (768, 384)])
    n_sub_total = DEPTH * 3
    for sgd in (sgA, sgB):
        for i in range(sgd.NT):
            if i < sgd.NP:
                g = sgd.ptiles[i]
                src = xp[g * 128:(g + 1) * 128, :]
            else:
                src = xs[:, :]
            P.dma("sp", "xin%d" % i, lambda e, i=i, src=src: e.dma_start(out=xres[:, i, :], in_=src),
                  writes=[("xres", i)])
            emit_xT(i)
        nsub = 0
        stop = False
        for l in range(DEPTH):
            for sub in range(3):
                if DEBUG_STOP is not None and nsub >= DEBUG_STOP:
                    stop = True
                    break
                is_last = (l == DEPTH - 1 and sub == 2)
                P.fence("BIGL")
                P.fence("WRL")
                if sub == 0:
                    ffn(sgd, l, 0, False)
                elif sub == 1:
                    if l % 2 == 0:
                        conv(sgd, l)
                    else:
                        sgu(sgd, l)
                else:
                    ffn(sgd, l, 1, is_last)
                nsub += 1
            if stop:
                break
        if stop:
            for i in range(sgd.NT):
                P.dma("sp", "yo%d" % i, lambda e, i=i, sgd=sgd: e.dma_start(out=out_rows(sgd, i), in_=xres[:, i, :]),
                      reads=[("xres", i)])
    P.flush()
    for sname in sorted(P.is_dma_sem):
        if sname.startswith(("yo", "svo", "sto")):
            P.final_wait("sp", sname)
    P.emit()
    stack.close()
    return nc


_CONSTS = None


def _consts():
    global _CONSTS
    if _CONSTS is None:
        t = np.arange(128)
        ident = np.eye(128, dtype=np.float32)
        tril = (t[None, :] <= t[:, None]).astype(np.float32)
        blk = ((t[None, :] // 8 == t[:, None] // 8) & (t[None, :] <= t[:, None])).astype(np.float32)
        _CONSTS = (ident, tril, blk)
    return _CONSTS


def kernel(x_prompt, x_sample, cache_conv, ln_g, ln_b, ffn1_w_in, ffn1_w_out, ffn2_w_in, ffn2_w_out,
           conv_w_in, conv_w, conv_w_out, sgu_w_in, sgu_ln_g, sgu_ln_b, sgu_w_s, sgu_b_s, sgu_w_out):
    f = lambda a: np.ascontiguousarray(np.asarray(a, dtype=np.float32))
    x_prompt, x_sample, cache_conv = f(x_prompt), f(x_sample), f(cache_conv)
    shared = {
        "ln_g": f(ln_g), "ln_b": f(ln_b), "ffn1_w_in": f(ffn1_w_in), "ffn1_w_out": f(ffn1_w_out),
        "ffn2_w_in": f(ffn2_w_in), "ffn2_w_out": f(ffn2_w_out), "conv_w_in": f(conv_w_in), "conv_w": f(conv_w),
        "conv_w_out": f(conv_w_out), "sgu_w_in": f(sgu_w_in), "sgu_ln_g": f(sgu_ln_g), "sgu_ln_b": f(sgu_ln_b),
        "sgu_w_s": f(sgu_w_s), "sgu_b_s": f(sgu_b_s), "sgu_w_out": f(sgu_w_out),
    }
    ident, tril, blk = _consts()
    shared.update({"c_ident": ident, "c_tril": tril, "c_blk": blk})
    in_maps = []
    for c in range(NCORES):
        m = dict(shared)
        m["xp"] = x_prompt[c]
        m["xs"] = np.ascontiguousarray(x_sample[16 * c:16 * (c + 1)].reshape(128, D))
        m["cache"] = np.ascontiguousarray(cache_conv[:, 16 * c:16 * (c + 1)].reshape(2, 32, D))
        in_maps.append(m)
    nc = build_nc()
    res = run_bass_kernel_spmd(nc, in_maps, core_ids=list(range(NCORES)))
    r = res.results
    y_p = np.stack([r[c]["y_p"] for c in range(NCORES)], axis=0).reshape(8, 2048, D)
    y_s = np.concatenate([r[c]["y_s"].reshape(16, 8, D) for c in range(NCORES)], axis=0)
    st_cp = np.stack([r[c]["st_cp"] for c in range(NCORES)], axis=1).reshape(2, 8, 2, D)
    st_cs = np.concatenate([r[c]["st_cs"].reshape(2, 16, 2, D) for c in range(NCORES)], axis=1)
    st_sv = np.concatenate([r[c]["st_sv"].reshape(2, 16, 8, DSGU) for c in range(NCORES)], axis=1)
    return (y_p.astype(np.float32), y_s.astype(np.float32), st_cp.astype(np.float32),
            st_cs.astype(np.float32), st_sv.astype(np.float32))
```

```python
import numpy as np
from contextlib import ExitStack
import concourse.bass as bass
import concourse.mybir as mybir
from concourse.bass_utils import run_bass_kernel_spmd

F32 = mybir.dt.float32
BF16 = mybir.dt.bfloat16
AF = mybir.ActivationFunctionType
ALU = mybir.AluOpType

D = 1024
DEPTH = 4
DFF = 2816
NJ = DFF // 128
DSGU = 3072
ALPHA = (2.0 * DEPTH) ** 0.25
EPS = 1e-5
NCORES = 8
MAXNT = 9
MAXTOK = MAXNT * 128

DEFER_N = 26.0
GATE_ENG = "pool"
DEBUG_STOP = None


class Prog:
    ENGS = ("pe", "act", "dve", "pool", "sp")

    def __init__(self, nc, stack):
        self.nc = nc
        self.stack = stack
        self.streams = {e: [] for e in self.ENGS}
        self.sems = {}
        self.cnt = {}
        self.is_dma_sem = set()
        self.last_w = {}
        self.readers = {}
        self.base = {}
        self.pending = []
        for e in self.ENGS:
            self._sem("E_" + e)

    def defer(self, countdown, write_keys, emit_fn):
        self.pending.append([countdown, set(write_keys), emit_fn])

    def flush(self, upto=None):
        n = len(self.pending) if upto is None else upto + 1
        todo, self.pending = self.pending[:n], self.pending[n:]
        for _, _, fn in todo:
            fn()

    def _check_pending(self, reads, writes):
        if not self.pending:
            return
        keys = set(reads) | set(writes)
        hit = -1
        for idx, (_, wk, _) in enumerate(self.pending):
            if wk & keys:
                hit = idx
        if hit >= 0:
            self.flush(hit)

    def _sem(self, name):
        if name not in self.sems:
            self.sems[name] = self.stack.enter_context(self.nc.semaphore(name))
            self.cnt[name] = 0
        return self.sems[name]

    def _deps(self, reads, writes):
        deps = {}

        def add(tok):
            if tok is None:
                return
            s, v = tok
            if deps.get(s, 0) < v:
                deps[s] = v

        for k in reads:
            add(self.last_w.get(k))
            for s, v in self.base.get(k, {}).items():
                add((s, v))
        for k in writes:
            add(self.last_w.get(k))
            for s, v in self.readers.get(k, {}).items():
                add((s, v))
            for s, v in self.base.get(k, {}).items():
                add((s, v))
        return deps

    def _commit(self, tok, reads, writes):
        s, v = tok
        for k in reads:
            r = self.readers.setdefault(k, {})
            if r.get(s, 0) < v:
                r[s] = v
        for k in writes:
            self.last_w[k] = tok
            self.readers[k] = {}
            self.base.pop(k, None)

    def fence(self, k):
        b = dict(self.base.get(k, {}))
        lw = self.last_w.pop(k, None)
        if lw is not None:
            s, v = lw
            b[s] = max(b.get(s, 0), v)
        for s, v in self.readers.pop(k, {}).items():
            b[s] = max(b.get(s, 0), v)
        self.base[k] = b

    def op(self, eng, fn, reads=(), writes=(), tick=True, cost=0.5):
        self._check_pending(reads, writes)
        deps = self._deps(reads, writes)
        name = "E_" + eng
        if eng == "pe":
            deps.pop(name, None)
        self.cnt[name] += 1
        tok = (name, self.cnt[name])
        self.streams[eng].append((deps, fn, (name, 1)))
        self._commit(tok, reads, writes)
        if eng == "pe" and tick and self.pending:
            for p in self.pending:
                p[0] -= cost
            while self.pending and self.pending[0][0] <= 0:
                self.flush(0)

    def dma(self, queue, sem, fn, reads=(), writes=()):
        self._check_pending(reads, writes)
        self._sem(sem)
        self.is_dma_sem.add(sem)
        deps = self._deps(reads, writes)
        self.cnt[sem] += 16
        tok = (sem, self.cnt[sem])
        self.streams[queue].append((deps, fn, (sem, 16)))
        self._commit(tok, reads, writes)

    def final_wait(self, queue, sem):
        self.streams[queue].append(({sem: self.cnt[sem]}, None, None))

    def emit(self):
        nc = self.nc
        with nc.Block() as block:
            def run(ename):
                def body(e):
                    waited = {}
                    for deps, fn, inc in self.streams[ename]:
                        for s, v in deps.items():
                            if waited.get(s, 0) >= v:
                                continue
                            e.wait_ge(self.sems[s], v)
                            waited[s] = v
                        if fn is None:
                            continue
                        ins = fn(e)
                        ins.then_inc(self.sems[inc[0]], inc[1])
                return body
            if self.streams["pe"]:
                block.tensor(run("pe"))
            if self.streams["act"]:
                block.scalar(run("act"))
            if self.streams["dve"]:
                block.vector(run("dve"))
            if self.streams["pool"]:
                block.gpsimd(run("pool"))
            if self.streams["sp"]:
                block.sync(run("sp"))


class SG:
    def __init__(self, name, ptiles, has_sample, groups):
        self.name = name
        self.ptiles = ptiles
        self.NP = len(ptiles)
        self.has_sample = has_sample
        self.NT = self.NP + (1 if has_sample else 0)
        self.TOK = self.NT * 128
        self.PL = self.NP * 128
        self.groups = groups
        assert sum(g[1] for g in groups) == self.TOK


def build_nc():
    nc = bass.Bass("TRN2", target_bir_lowering=False)
    stack = ExitStack()

    def din(name, shape):
        return nc.dram_tensor(name, list(shape), F32, kind="ExternalInput").ap()

    def dout(name, shape):
        return nc.dram_tensor(name, list(shape), F32, kind="ExternalOutput").ap()

    xp = din("xp", (2048, D))
    xs = din("xs", (128, D))
    cache = din("cache", (2, 32, D))
    ln_g = din("ln_g", (DEPTH, 3, D))
    ln_b = din("ln_b", (DEPTH, 3, D))
    f1_in = din("ffn1_w_in", (DEPTH, D, 2 * DFF))
    f1_out = din("ffn1_w_out", (DEPTH, DFF, D))
    f2_in = din("ffn2_w_in", (DEPTH, D, 2 * DFF))
    f2_out = din("ffn2_w_out", (DEPTH, DFF, D))
    cv_in = din("conv_w_in", (2, D, 3 * D))
    cv_w = din("conv_w", (2, 3, D))
    cv_out = din("conv_w_out", (2, D, D))
    sg_in = din("sgu_w_in", (2, D, 2 * DSGU))
    sg_g = din("sgu_ln_g", (2, DSGU))
    sg_b = din("sgu_ln_b", (2, DSGU))
    sg_ws = din("sgu_w_s", (2, 8, 128, 128))
    sg_bs = din("sgu_b_s", (2, 8, 128))
    sg_out = din("sgu_w_out", (2, DSGU, D))
    c_ident = din("c_ident", (128, 128))
    c_tril = din("c_tril", (128, 128))
    c_blk = din("c_blk", (128, 128))

    y_p = dout("y_p", (2048, D))
    y_s = dout("y_s", (128, D))
    st_cp = dout("st_cp", (2, 2, D))
    st_cs = dout("st_cs", (2, 32, D))
    st_sv = dout("st_sv", (2, 128, DSGU))

    def sb(name, shape, dt):
        return stack.enter_context(nc.sbuf_tensor(name, list(shape), dt))

    xres = sb("xres", (128, MAXNT, D), F32)
    xT = sb("xT", (128, 8, MAXTOK), BF16)
    BIG_F32 = 27712
    big = sb("big", (128, BIG_F32), F32)
    wring = sb("wring", (128, 8192), BF16)
    lnp = sb("lnp", (128, 2, D), F32)
    tbuf = sb("tbuf", (128, 2, D), F32)
    xbf = sb("xbf", (128, 2, D), BF16)
    sgt = sb("sgt", (128, 2, 512), F32)
    stt = sb("stt", (128, 2, 40), F32)
    ident_f = sb("ident_f", (128, 128), F32)
    ident_b = sb("ident_b", (128, 128), BF16)
    tril_f = sb("tril_f", (128, 128), F32)
    blk_f = sb("blk_f", (128, 128), F32)
    ones_b = sb("ones_b", (128, 128), BF16)
    eps_t = sb("eps_t", (128, 8), F32)
    colp = sb("colp", (128, 144), F32)
    pnat = sb("pnat", (128, 2, 128), F32)
    hist = sb("hist", (128, 2, 8, 2), F32)
    ps = [stack.enter_context(nc.psum_tensor("ps%d" % i, [128, 512], F32)) for i in range(8)]

    P = Prog(nc, stack)
    PS7 = [("ps7m", 0), ("ps7m", 1)]

    def pk(q):
        return list(PS7) if q == 7 else ["ps%d" % q]

    def bigv(off_f32, n_f32, dt=F32):
        v = big[:, off_f32:off_f32 + n_f32]
        if dt == BF16:
            v = v.bitcast(BF16)
        return v

    P.dma("sp", "c0", lambda e: e.dma_start(out=ident_f[:], in_=c_ident[:, :]), writes=["ident_f"])
    P.dma("sp", "c1", lambda e: e.dma_start(out=tril_f[:], in_=c_tril[:, :]), writes=["tril_f"])
    P.dma("sp", "c2", lambda e: e.dma_start(out=blk_f[:], in_=c_blk[:, :]), writes=["blk_f"])
    P.op("act", lambda e: e.copy(out=ident_b[:], in_=ident_f[:]), reads=["ident_f"], writes=["ident_b"])
    P.op("dve", lambda e: e.memset(ones_b[:], 1.0), writes=["ones_b"])
    P.op("dve", lambda e: e.memset(eps_t[:], EPS), writes=["eps_t"])
    P.dma("sp", "c3", lambda e: e.dma_start(
        out=pnat[0:48, 0, :], in_=cv_w.rearrange("j k (c p) -> (j k c) p", p=128)), writes=["pnat0"])
    P.dma("sp", "c3", lambda e: e.dma_start(
        out=pnat[48:96, 0, :], in_=sg_g.rearrange("j (c p) -> (j c) p", p=128)), writes=["pnat1"])
    P.dma("sp", "c3", lambda e: e.dma_start(
        out=pnat[0:48, 1, :], in_=sg_b.rearrange("j (c p) -> (j c) p", p=128)), writes=["pnat2"])

    def _tp(e):
        e.transpose(out=ps[6][:, 0:96], in_=pnat[0:96, 0, :], identity=ident_f[0:96, 0:96])
        return e.transpose(out=ps[6][:, 96:144], in_=pnat[0:48, 1, :], identity=ident_f[0:48, 0:48])
    P.op("pe", _tp, reads=["pnat0", "pnat1", "pnat2", "ident_f"], writes=["ps6"])
    P.op("act", lambda e: e.copy(out=colp[:, :], in_=ps[6][:, 0:144]), reads=["ps6"], writes=["colp"])

    def cw_col(j, k, c):
        i = (j * 3 + k) * 8 + c
        return colp[:, i:i + 1]

    def sgg_col(j, c):
        i = 48 + j * 24 + c
        return colp[:, i:i + 1]

    def sgb_col(j, c):
        i = 96 + j * 24 + c
        return colp[:, i:i + 1]

    xbf_ctr = [0]
    t_ctr = [0]
    sg_ctr = [0]

    def emit_xT(i, defer=True):
        b = xbf_ctr[0] % 2
        xbf_ctr[0] += 1
        P.op("act", lambda e: e.copy(out=xbf[:, b, :], in_=xres[:, i, :]),
             reads=[("xres", i)], writes=[("xbf", b)])
        tbank = 6 if b == 0 else 7
        tb = ps[tbank][:, :].bitcast(BF16)

        def _t(e):
            ins = None
            for k in range(8):
                ins = e.transpose(out=tb[:, k * 128:(k + 1) * 128], in_=xbf[:, b, k * 128:(k + 1) * 128],
                                  identity=ident_b[:, :])
            return ins

        def emit_tail():
            P.op("pe", _t, reads=[("xbf", b), "ident_b"], writes=pk(tbank), tick=False)
            P.op("act", lambda e: e.copy(out=xT[:, :, i * 128:(i + 1) * 128],
                                         in_=tb.rearrange("p (k t) -> p k t", k=8)),
                 reads=pk(tbank), writes=[("xT", i)])
        if defer:
            P.defer(DEFER_N, [("xT", i), ("xbf", b)] + pk(tbank), emit_tail)
        else:
            emit_tail()

    def load_lnp(l, s):
        P.dma("sp", "lnp", lambda e: e.dma_start(out=lnp[:, 0, :], in_=ln_g[l, s, :].partition_broadcast(128)),
              writes=["lnp0"])
        P.dma("sp", "lnp", lambda e: e.dma_start(out=lnp[:, 1, :], in_=ln_b[l, s, :].partition_broadcast(128)),
              writes=["lnp1"])

    def out_rows(sgd, i):
        if i < sgd.NP:
            g = sgd.ptiles[i]
            return y_p[g * 128:(g + 1) * 128, :]
        return y_s[:, :]

    epi_pending = []

    def flush_epi():
        while epi_pending:
            epi_pending.pop(0)()

    def epilogue(sgd, i, y0, y1, x_scale, is_last, in_xres=None):
        b = t_ctr[0] % 2
        t_ctr[0] += 1
        st = stt[:, b, :]
        for h, yb in enumerate((y0, y1)):
            P.op("dve", lambda e, h=h, yb=yb: e.scalar_tensor_tensor(
                out=tbuf[:, b, h * 512:(h + 1) * 512], in0=xres[:, i, h * 512:(h + 1) * 512], scalar=float(x_scale),
                in1=ps[yb][:, :], op0=ALU.mult, op1=ALU.add),
                reads=[("xres", i), "ps%d" % yb], writes=[("t", b, h)])
        for h in range(2):
            P.op("dve", lambda e, h=h: e.bn_stats(out=st[:, h * 6:(h + 1) * 6], in_=tbuf[:, b, h * 512:(h + 1) * 512]),
                 reads=[("t", b, h)], writes=[("st", b, h)])
        P.op("dve", lambda e: e.bn_aggr(out=st[:, 12:14], in_=st[:, 0:12].rearrange("p (n t) -> p n t", t=3)),
             reads=[("st", b, 0), ("st", b, 1)], writes=[("mv", b)])
        P.op("act", lambda e: e.activation(out=st[:, 16:17], in_=st[:, 13:14], func=AF.Ln, bias=eps_t[:, 0:1], scale=1.0),
             reads=[("mv", b), "eps_t"], writes=[("lnv", b)])
        P.op("act", lambda e: e.activation(out=st[:, 14:15], in_=st[:, 16:17], func=AF.Exp, scale=-0.5),
             reads=[("lnv", b)], writes=[("rstd", b)])
        P.op("dve", lambda e: e.tensor_scalar(out=st[:, 17:18], in0=st[:, 12:13], scalar1=-1.0, scalar2=None,
                                              op0=ALU.mult),
             reads=[("mv", b)], writes=[("negm", b)])
        P.op("act", lambda e: e.activation(out=st[:, 15:16], in_=st[:, 14:15], func=AF.Identity, scale=st[:, 17:18]),
             reads=[("rstd", b), ("negm", b)], writes=[("nmr", b)])
        P.op("act", lambda e: e.activation(out=tbuf[:, b, :], in_=tbuf[:, b, :], func=AF.Identity,
                                           bias=st[:, 15:16], scale=st[:, 14:15]),
             reads=[("t", b, 0), ("t", b, 1), ("rstd", b), ("nmr", b)], writes=[("t", b, 0), ("t", b, 1)])

        def stage_b():
            P.op("dve", lambda e: e.tensor_tensor(out=tbuf[:, b, :], in0=tbuf[:, b, :], in1=lnp[:, 0, :], op=ALU.mult),
                 reads=[("t", b, 0), ("t", b, 1), "lnp0", "lnp1"], writes=[("t", b, 0), ("t", b, 1), "lnp0", "lnp1"])
            P.op("dve", lambda e: e.tensor_tensor(out=xres[:, i, :], in0=tbuf[:, b, :], in1=lnp[:, 1, :], op=ALU.add),
                 reads=[("t", b, 0), ("t", b, 1), "lnp0", "lnp1"], writes=[("xres", i), "lnp0", "lnp1"])
            if is_last:
                P.dma("sp", "yo%d" % i, lambda e: e.dma_start(out=out_rows(sgd, i), in_=xres[:, i, :]),
                      reads=[("xres", i)])
            else:
                emit_xT(i)
        flush_epi()
        epi_pending.append(stage_b)

    def out_proj(sgd, i, n_chunks, lhs_fn, rhs_fn, key_reads, key_writes=()):
        for n, yb in enumerate((4, 5)):
            def _m(e, n=n, yb=yb):
                ins = None
                for c in range(n_chunks):
                    ins = e.matmul(ps[yb][:, :], lhsT=lhs_fn(c, i), rhs=rhs_fn(c)[:, n * 512:(n + 1) * 512],
                                   start=(c == 0), stop=(c == n_chunks - 1))
                return ins
            P.op("pe", _m, reads=key_reads, writes=["ps%d" % yb] + list(key_writes), cost=n_chunks * 512 / 1950.0)

    def ffn(sgd, l, which, is_last):
        w_in = (f1_in, f2_in)[which][l].rearrange("(k p) n -> p k n", p=128)
        w_out = (f1_out, f2_out)[which][l].rearrange("(j p) n -> p j n", p=128)
        TOK = sgd.TOK
        hT = bigv(0, NJ * MAXTOK // 2, BF16).rearrange("p (j t) -> p j t", j=NJ)
        wo_off = NJ * MAXTOK // 2
        wout = bigv(wo_off, NJ * D // 2, BF16).rearrange("p (j n) -> p j n", j=NJ)
        assert wo_off + NJ * D // 2 <= BIG_F32
        ws4 = wring[:, :].rearrange("p (s k n) -> p s k n", s=4, k=8)
        sp_off = wo_off + NJ * D // 2
        ws3 = bigv(sp_off, 3 * 1024, BF16).rearrange("p (s k n) -> p s k n", s=3, k=8)
        assert sp_off + 3 * 1024 <= BIG_F32
        NS = 7
        JB = 4 if len(sgd.groups) == 2 else 3

        def wsl(sl):
            return ws4[:, sl] if sl < 4 else ws3[:, sl - 4]
        wo_piece = [0]

        def issue_wout():
            j0 = wo_piece[0]
            if j0 >= NJ:
                return
            j1 = min(NJ, j0 + 2)
            wo_piece[0] = j1
            P.dma("pool", "wout", lambda e: e.dma_start(out=wout[:, j0:j1, :], in_=w_out[:, j0:j1, :]),
                  reads=["BIGL"], writes=[("wout", j) for j in range(j0, j1)])

        for j0 in range(0, NJ, JB):
            blk = list(range(j0, min(NJ, j0 + JB)))
            for j in blk:
                sl = j % NS
                P.dma("pool", "wr%d" % sl, lambda e, sl=sl, j=j: e.dma_start(
                    out=wsl(sl)[:, :, 0:128], in_=w_in[:, :, j * 128:(j + 1) * 128]),
                    reads=["WRL"] + (["BIGL"] if sl >= 4 else []), writes=[("wr", sl, 0)])
                P.dma("pool", "wr%d" % sl, lambda e, sl=sl, j=j: e.dma_start(
                    out=wsl(sl)[:, :, 128:256], in_=w_in[:, :, DFF + j * 128:DFF + (j + 1) * 128]),
                    reads=["WRL"] + (["BIGL"] if sl >= 4 else []), writes=[("wr", sl, 1)])
                if j >= NS - 1:
                    issue_wout()
            for (g0, gs) in sgd.groups:
                for j in blk:
                    sl = j % NS
                    b = sg_ctr[0] % 2
                    sg_ctr[0] += 1
                    gb, ub = (0, 2) if b == 0 else (1, 3)

                    def _m(e, sl=sl, g0=g0, gs=gs, gb=gb, ub=ub):
                        ins = None
                        for half, bank in ((0, gb), (1, ub)):
                            for k in range(8):
                                ins = e.matmul(ps[bank][:, 0:gs], lhsT=wsl(sl)[:, k, half * 128:(half + 1) * 128],
                                               rhs=xT[:, k, g0:g0 + gs], start=(k == 0), stop=(k == 7))
                        return ins
                    tiles = range(g0 // 128, (g0 + gs) // 128)
                    P.op("pe", _m, reads=["WRL", "BIGL", ("wr", sl, 0), ("wr", sl, 1)] + [("xT", i) for i in tiles],
                         writes=["ps%d" % gb, "ps%d" % ub, ("wr", sl, 0), ("wr", sl, 1)], cost=16 * gs / 1950.0)
                    P.op("act", lambda e, b=b, gb=gb, gs=gs: e.activation(out=sgt[:, b, 0:gs], in_=ps[gb][:, 0:gs],
                                                                           func=AF.Silu),
                         reads=["ps%d" % gb], writes=[("sgt", b)])
                    P.op("dve", lambda e, b=b, ub=ub, j=j, g0=g0, gs=gs: e.scalar_tensor_tensor(
                        out=hT[:, j, g0:g0 + gs], in0=sgt[:, b, 0:gs], scalar=0.5, in1=ps[ub][:, 0:gs],
                        op0=ALU.mult, op1=ALU.mult),
                        reads=[("sgt", b), "ps%d" % ub, "BIGL"], writes=[("hT", j)])
        while wo_piece[0] < NJ:
            issue_wout()
        load_lnp(l, 0 if which == 0 else 2)
        for i in range(sgd.NT):
            out_proj(sgd, i, NJ, lambda c, i: hT[:, c, i * 128:(i + 1) * 128], lambda c: wout[:, c, :],
                     [("hT", j) for j in range(NJ)] + [("wout", j) for j in range(NJ)] + ["BIGL"],
                     [("wout", j) for j in range(NJ)])
            epilogue(sgd, i, 4, 5, ALPHA, is_last)
        flush_epi()

    def conv(sgd, l):
        j = l // 2
        w_in = cv_in[j].rearrange("(k p) n -> p k n", p=128)
        w_out = cv_out[j].rearrange("(c p) n -> p c n", p=128)
        TOK, PL = sgd.TOK, sgd.PL
        off = 0
        yT = bigv(off, 8 * MAXTOK // 2, BF16).rearrange("p (c t) -> p c t", c=8); off += 8 * MAXTOK // 2
        wout = bigv(off, 8 * D // 2, BF16).rearrange("p (c n) -> p c n", c=8); off += 8 * D // 2
        gbuf = bigv(off, 2 * 1032).rearrange("p (b t) -> p b t", b=2); off += 2 * 1032
        gsb = bigv(off, 2 * 160).rearrange("p (b a t) -> p b a t", b=2, a=16); off += 2 * 160
        bT = bigv(off, 2 * MAXTOK).rearrange("p (b t) -> p b t", b=2); off += 2 * MAXTOK
        cvb = bigv(off, MAXTOK); off += MAXTOK
        stT = bigv(off, 8 * 34).rearrange("p (c r) -> p c r", c=8); off += 8 * 34
        cacheT = bigv(off, 8 * 32).rearrange("p (c r) -> p c r", c=8); off += 8 * 32
        cnat = bigv(off, D); off += D
        stout = bigv(off, D); off += D
        assert off <= BIG_F32
        wslots = wring[:, 0:2 * 8 * 384].rearrange("p (s k n) -> p s k n", s=2, k=8)
        load_lnp(l, 1)
        def issue_cwout():
            P.dma("pool", "wout", lambda e: e.dma_start(out=wout[:, :, :], in_=w_out[:, :, :]),
                  reads=["BIGL"], writes=["cwout"])
        if sgd.has_sample:
            P.dma("sp", "cnat", lambda e: e.dma_start(out=cnat[0:32, :], in_=cache[j, :, :]),
                  reads=["BIGL"], writes=["cnat"])

            def _tc(e):
                ins = None
                for c in range(8):
                    ins = e.transpose(out=ps[7][:, c * 32:(c + 1) * 32], in_=cnat[0:32, c * 128:(c + 1) * 128],
                                      identity=ident_f[0:32, 0:32])
                return ins
            P.op("pe", _tc, reads=["cnat", "ident_f", "BIGL"], writes=PS7)
            P.op("act", lambda e: e.copy(out=cacheT[:, :, :], in_=ps[7][:, 0:256].rearrange("p (c r) -> p c r", c=8)),
                 reads=PS7 + ["BIGL"], writes=["cacheT"])
        for c in range(8):
            s = c % 2
            cb = c % 2
            for part in range(3):
                P.dma("pool", "wr%d" % s, lambda e, s=s, c=c, part=part: e.dma_start(
                    out=wslots[:, s, :, part * 128:(part + 1) * 128],
                    in_=w_in[:, :, part * D + c * 128:part * D + (c + 1) * 128]), reads=["WRL"], writes=[("wr", s, part)])
            if c == 1:
                issue_cwout()
            for (g0, gs) in sgd.groups:
                b = sg_ctr[0] % 2
                sg_ctr[0] += 1
                banks = (0, 1, 2) if b == 0 else (3, 7, 6)

                def _m(e, s=s, g0=g0, gs=gs, banks=banks):
                    ins = None
                    for part in range(3):
                        for k in range(8):
                            ins = e.matmul(ps[banks[part]][:, 0:gs], lhsT=wslots[:, s, k, part * 128:(part + 1) * 128],
                                           rhs=xT[:, k, g0:g0 + gs], start=(k == 0), stop=(k == 7))
                    return ins
                tiles = range(g0 // 128, (g0 + gs) // 128)
                P.op("pe", _m, reads=["WRL", ("wr", s, 0), ("wr", s, 1), ("wr", s, 2)] + [("xT", i) for i in tiles],
                     writes=sum([pk(q) for q in banks], []) + [("wr", s, 0), ("wr", s, 1), ("wr", s, 2)],
                     cost=24 * gs / 1950.0)
                bb, cbk, hb = banks
                P.op("act", lambda e, b=b, cbk=cbk, gs=gs: e.copy(out=sgt[:, b, 0:gs], in_=ps[cbk][:, 0:gs]),
                     reads=pk(cbk), writes=[("sgt", b)])
                p1 = min(g0 + gs, PL)
                if g0 < p1:
                    n = p1 - g0
                    P.op("dve", lambda e, b=b, hb=hb, cb=cb, g0=g0, n=n: e.tensor_tensor(
                        out=gbuf[:, cb, 2 + g0:2 + g0 + n], in0=sgt[:, b, 0:n], in1=ps[hb][:, 0:n], op=ALU.mult),
                        reads=[("sgt", b), "BIGL"] + pk(hb), writes=[("gbuf", cb)])
                if g0 + gs > PL:
                    o = PL - g0
                    P.op("dve", lambda e, b=b, hb=hb, cb=cb, o=o: e.tensor_tensor(
                        out=gsb[:, cb, :, 2:10], in0=sgt[:, b, o:o + 128].rearrange("p (a t) -> p a t", a=16),
                        in1=ps[hb][:, o:o + 128].rearrange("p (a t) -> p a t", a=16), op=ALU.mult),
                        reads=[("sgt", b), "BIGL"] + pk(hb), writes=[("gsb", cb)])
                P.op("act", lambda e, bb=bb, cb=cb, g0=g0, gs=gs: e.copy(out=bT[:, cb, g0:g0 + gs], in_=ps[bb][:, 0:gs]),
                     reads=pk(bb) + ["BIGL"], writes=[("bT", cb)])
            if sgd.name == "A":
                P.op("dve", lambda e, cb=cb: e.memset(gbuf[:, cb, 0:2], 0.0), reads=["BIGL"], writes=[("gbuf", cb)])
            else:
                P.op("dve", lambda e, cb=cb, c=c: e.tensor_copy(out=gbuf[:, cb, 0:2], in_=hist[:, j, c, :]),
                     reads=[("hist", j, c), "BIGL"], writes=[("gbuf", cb)])
            if sgd.has_sample:
                P.op("dve", lambda e, cb=cb, c=c: e.tensor_copy(
                    out=gsb[:, cb, :, 0:2], in_=cacheT[:, c, :].rearrange("p (a r) -> p a r", a=16)),
                    reads=["cacheT", "BIGL"], writes=[("gsb", cb)])
            rk = [("gbuf", cb), ("gsb", cb), "colp", "BIGL"]
            P.op("dve", lambda e, cb=cb, c=c: e.tensor_scalar(out=cvb[:, 0:PL], in0=gbuf[:, cb, 0:PL],
                                                             scalar1=cw_col(j, 0, c), scalar2=None, op0=ALU.mult),
                 reads=rk, writes=["cvb"])
            for k in (1, 2):
                P.op("dve", lambda e, cb=cb, c=c, k=k: e.scalar_tensor_tensor(
                    out=cvb[:, 0:PL], in0=gbuf[:, cb, k:k + PL], scalar=cw_col(j, k, c), in1=cvb[:, 0:PL],
                    op0=ALU.mult, op1=ALU.add), reads=rk + ["cvb"], writes=["cvb"])
            if sgd.has_sample:
                cvs = cvb[:, PL:PL + 128].rearrange("p (a t) -> p a t", a=16)
                P.op("dve", lambda e, cb=cb, c=c: e.tensor_scalar(out=cvs, in0=gsb[:, cb, :, 0:8],
                                                                 scalar1=cw_col(j, 0, c), scalar2=None, op0=ALU.mult),
                     reads=rk, writes=["cvs"])
                for k in (1, 2):
                    P.op("dve", lambda e, cb=cb, c=c, k=k: e.scalar_tensor_tensor(
                        out=cvs, in0=gsb[:, cb, :, k:k + 8], scalar=cw_col(j, k, c), in1=cvs,
                        op0=ALU.mult, op1=ALU.add), reads=rk + ["cvs"], writes=["cvs"])
            P.op("dve", lambda e, cb=cb, c=c: e.tensor_tensor(out=yT[:, c, 0:TOK], in0=cvb[:, 0:TOK],
                                                             in1=bT[:, cb, 0:TOK], op=ALU.mult),
                 reads=["cvb", "cvs", ("bT", cb), "BIGL"], writes=[("yT", c)])
            if sgd.name == "A":
                P.op("dve", lambda e, cb=cb, c=c: e.tensor_copy(out=hist[:, j, c, :], in_=gbuf[:, cb, PL:PL + 2]),
                     reads=[("gbuf", cb), "BIGL"], writes=[("hist", j, c)])
            else:
                P.op("dve", lambda e, cb=cb, c=c: e.tensor_copy(
                    out=stT[:, c, 0:32].rearrange("p (a r) -> p a r", a=16), in_=gsb[:, cb, :, 8:10]),
                    reads=[("gsb", cb), "BIGL"], writes=[("stT", c)])
                P.op("dve", lambda e, cb=cb, c=c: e.tensor_copy(out=stT[:, c, 32:34], in_=gbuf[:, cb, PL:PL + 2]),
                     reads=[("gbuf", cb), "BIGL"], writes=[("stT", c)])
        if sgd.name == "B":
            for half, bank in ((0, 0), (1, 1)):
                def _ts(e, half=half, bank=bank):
                    ins = None
                    for q in range(4):
                        c = half * 4 + q
                        ins = e.transpose(out=ps[bank][0:34, q * 128:(q + 1) * 128], in_=stT[:, c, :],
                                          identity=ident_f[:, :])
                    return ins
                P.op("pe", _ts, reads=[("stT", c) for c in range(8)] + ["ident_f", "BIGL"], writes=["ps%d" % bank])
                P.op("act", lambda e, half=half, bank=bank: e.copy(out=stout[0:34, half * 512:(half + 1) * 512],
                                                                   in_=ps[bank][0:34, :]),
                     reads=["ps%d" % bank, "BIGL"], writes=[("stout", half)])
            P.dma("sp", "sto", lambda e: e.dma_start(out=st_cs[j, :, :], in_=stout[0:32, :]),
                  reads=[("stout", 0), ("stout", 1), "BIGL"])
            P.dma("sp", "sto", lambda e: e.dma_start(out=st_cp[j, :, :], in_=stout[32:34, :]),
                  reads=[("stout", 0), ("stout", 1), "BIGL"])
        for i in range(sgd.NT):
            out_proj(sgd, i, 8, lambda c, i: yT[:, c, i * 128:(i + 1) * 128], lambda c: wout[:, c, :],
                     [("yT", c) for c in range(8)] + ["cwout", "BIGL"])
            epilogue(sgd, i, 4, 5, ALPHA, False)
        flush_epi()

    def sgu(sgd, l):
        j = l // 2
        w_in = sg_in[j].rearrange("(k p) n -> p k n", p=128)
        w_out = sg_out[j].rearrange("(c p) n -> p c n", p=128)
        TOK, PL, NT = sgd.TOK, sgd.PL, sgd.NT
        off = 0
        vraw = bigv(off, MAXNT * DSGU // 2, BF16).rearrange("p (i f) -> p i f", i=MAXNT); off += MAXNT * DSGU // 2
        uT = bigv(off, 6 * MAXTOK // 2, BF16).rearrange("p (c t) -> p c t", c=6); off += 6 * MAXTOK // 2
        wout = bigv(off, 6 * D // 2, BF16).rearrange("p (c n) -> p c n", c=6); off += 6 * D // 2
        WTp = bigv(off, 8 * 128 // 2, BF16).rearrange("p (h t) -> p h t", h=8); off += 8 * 128 // 2
        WTs = bigv(off, 8 * 128 // 2, BF16).rearrange("p (h t) -> p h t", h=8); off += 8 * 128 // 2
        bbp = bigv(off, 8 * 128).rearrange("p (h t) -> p h t", h=8); off += 8 * 128
        B2p = bigv(off, 6 * 128).rearrange("p (c t) -> p c t", c=6); off += 6 * 128
        B2s = bigv(off, 6 * 128).rearrange("p (c t) -> p c t", c=6); off += 6 * 128
        wnb = bigv(off, 8 * 128 // 2, BF16).rearrange("p (h t) -> p h t", h=8)
        vn = bigv(off, 2 * 768 // 2, BF16).rearrange("p (b f) -> p b f", b=2); off += 2 * 768 // 2
        wnat = bigv(off, 8 * 128).rearrange("p (h t) -> p h t", h=8)
        mtmp = bigv(off, 2 * 768).rearrange("p (b c t) -> p b c t", b=2, c=6); off += 2 * 768
        svst = bigv(off, MAXNT * 48).rearrange("p (i q) -> p i q", i=MAXNT); off += MAXNT * 48
        assert off <= BIG_F32, off

        def prep_head(mask, sample):
            if not sample:
                P.dma("sp", "wnat", lambda e: e.dma_start(out=wnat[:, :, :], in_=sg_ws[j].rearrange("h t s -> t h s")),
                      reads=["BIGL", "PREPL"], writes=["wnat"] + [("wnatd", a) for a in range(16)])
            else:
                P.op("dve", lambda e: e.memset(wnat[:, :, :], 0.0), reads=["BIGL", "PREPL"],
                     writes=["wnat"] + [("wnatd", a) for a in range(16)])
                for a in range(16):
                    P.dma("sp", "wnat", lambda e, a=a: e.dma_start(
                        out=wnat[a * 8:(a + 1) * 8, :, a * 8:(a + 1) * 8],
                        in_=sg_ws[j, :, 0:8, 0:8].rearrange("h t s -> t h s")), reads=["PREPL", "BIGL", "wnat"],
                        writes=[("wnatd", a)])
            P.op("dve", lambda e: e.tensor_tensor(out=wnb[:, :, :], in0=wnat[:, :, :],
                                                  in1=mask[:, :].unsqueeze(1).to_broadcast([128, 8, 128]), op=ALU.mult),
                 reads=["wnat", "tril_f", "blk_f", "BIGL", "PREPL"] + [("wnatd", a) for a in range(16)],
                 writes=["wnb", "wnat"] + [("wnatd", a) for a in range(16)])

        def prep_tail(dst, sample):
            tb = ps[7][:, :].bitcast(BF16)

            def _t(e):
                ins = None
                for h in range(8):
                    ins = e.transpose(out=tb[:, h * 128:(h + 1) * 128], in_=wnb[:, h, :], identity=ident_b[:, :])
                return ins
            P.op("pe", _t, reads=["wnb", "ident_b", "PREPL", "BIGL"], writes=PS7 + ["wnb"])
            P.op("act", lambda e: e.copy(out=dst[:, :, :], in_=tb.rearrange("p (h t) -> p h t", h=8)),
                 reads=PS7 + ["BIGL"], writes=[("WT", sample)])

        prep_head(tril_f, False)
        P.dma("sp", "bb", lambda e: e.dma_start(
            out=bbp[:, :, :].rearrange("p h t -> p (h t)"),
            in_=sg_bs[j].rearrange("h t -> (h t)").partition_broadcast(128)), reads=["BIGL"], writes=["bbp"])

        vslots = wring[:, :].rearrange("p (s k n) -> p s k n", s=2, k=8)
        for f in range(6):
            s = f % 2
            P.dma("pool", "wr%d" % s, lambda e, s=s, f=f: e.dma_start(
                out=vslots[:, s, :, :], in_=w_in[:, :, DSGU + f * 512:DSGU + (f + 1) * 512]), reads=["WRL"], writes=[("wr", s, 0)])
            for i in range(NT):
                b = sg_ctr[0] % 2
                sg_ctr[0] += 1
                vb = b

                def _m(e, s=s, i=i, vb=vb):
                    ins = None
                    for k in range(8):
                        ins = e.matmul(ps[vb][:, :], lhsT=xT[:, k, i * 128:(i + 1) * 128], rhs=vslots[:, s, k, :],
                                       start=(k == 0), stop=(k == 7))
                    return ins
                P.op("pe", _m, reads=["WRL", ("wr", s, 0), ("xT", i)], writes=["ps%d" % vb], cost=8 * 512 / 1950.0)
                P.op("act", lambda e, b=b, vb=vb: e.activation(out=sgt[:, b, :], in_=ps[vb][:, :], func=AF.Gelu),
                     reads=["ps%d" % vb], writes=[("sgt", b)])
                P.op("dve", lambda e, b=b, i=i, f=f: e.bn_stats(out=svst[:, i, f * 6:(f + 1) * 6], in_=sgt[:, b, :]),
                     reads=[("sgt", b), "BIGL"], writes=[("svst", i, f)])
                P.op("act", lambda e, b=b, i=i, f=f: e.copy(out=vraw[:, i, f * 512:(f + 1) * 512], in_=sgt[:, b, :]),
                     reads=[("sgt", b), "BIGL"], writes=[("vraw", i)])
            if f == 0:
                prep_tail(WTp, False)
                if sgd.has_sample:
                    prep_head(blk_f, True)
            if f == 2 and sgd.has_sample:
                prep_tail(WTs, True)
        for i in range(NT):
            P.op("dve", lambda e, i=i: e.bn_aggr(out=svst[:, i, 36:38],
                                                 in_=svst[:, i, 0:36].rearrange("p (n t) -> p n t", t=3)),
                 reads=[("svst", i, f) for f in range(6)] + ["BIGL"], writes=[("smv", i)])
            P.op("act", lambda e, i=i: e.activation(out=svst[:, i, 40:41], in_=svst[:, i, 37:38], func=AF.Ln,
                                                    bias=eps_t[:, 0:1], scale=1.0),
                 reads=[("smv", i), "eps_t", "BIGL"], writes=[("slnv", i)])
            P.op("act", lambda e, i=i: e.activation(out=svst[:, i, 38:39], in_=svst[:, i, 40:41], func=AF.Exp, scale=-0.5),
                 reads=[("slnv", i), "BIGL"], writes=[("srstd", i)])
            P.op("dve", lambda e, i=i: e.scalar_tensor_tensor(out=svst[:, i, 39:40], in0=svst[:, i, 36:37], scalar=-1.0,
                                                              in1=svst[:, i, 38:39], op0=ALU.mult, op1=ALU.mult),
                 reads=[("smv", i), ("srstd", i), "BIGL"], writes=[("snmr", i)])

        load_lnp(l, 1)
        P.fence("PREPL")

        P.fence("WRL")
        uslots = wring[:, :].rearrange("p (s k n) -> p s k n", s=8, k=8)
        m_ctr = [0]

        def issue_wu(part):
            for cl in range(6):
                cg = part * 6 + cl
                s = cg % 8
                P.dma("pool", "wu%d" % s, lambda e, s=s, cg=cg: e.dma_start(
                    out=uslots[:, s, :, :], in_=w_in[:, :, cg * 128:(cg + 1) * 128]), reads=["WRL"], writes=[("wu", s)])
        for part in range(4):
            if sgd.has_sample and part < 3:
                for q in range(2):
                    f = 2 * part + q
                    P.dma("sp", "gb%d" % q, lambda e, q=q, f=f: e.dma_start(
                        out=tbuf[:, 0, q * 512:(q + 1) * 512], in_=sg_g[j, f * 512:(f + 1) * 512].partition_broadcast(128)),
                        writes=[("t", 0, q)])
                    P.dma("sp", "gb%d" % (2 + q), lambda e, q=q, f=f: e.dma_start(
                        out=tbuf[:, 1, q * 512:(q + 1) * 512], in_=sg_b[j, f * 512:(f + 1) * 512].partition_broadcast(128)),
                        writes=[("t", 1, q)])
            variants = [(0, WTp, B2p)] + ([(1, WTs, B2s)] if sgd.has_sample else [])
            for variant, WT, B2 in variants:
                for hh in range(2):
                    h = part * 2 + hh
                    r0 = (variant * 2 + hh) * 128
                    P.op("pe", lambda e, WT=WT, h=h, r0=r0: e.matmul(ps[7][:, r0:r0 + 128], lhsT=ones_b[:, :],
                                                                    rhs=WT[:, h, :], start=True, stop=True),
                         reads=[("WT", variant == 1), "ones_b", "BIGL"], writes=[("ps7m", variant)])
            for variant, WT, B2 in variants:
                for hh in range(2):
                    h = part * 2 + hh
                    r0 = (variant * 2 + hh) * 128
                    for q in range(3):
                        cl = hh * 3 + q
                        if variant == 0:
                            P.op("dve", lambda e, cl=cl, h=h, B2=B2, part=part, r0=r0: e.scalar_tensor_tensor(
                                out=B2[:, cl, :], in0=ps[7][:, r0:r0 + 128], scalar=sgb_col(j, part * 6 + cl),
                                in1=bbp[:, h, :], op0=ALU.mult, op1=ALU.add),
                                reads=[("ps7m", variant), "bbp", "colp", "BIGL"], writes=[("B2", variant)])
                        else:
                            P.op("dve", lambda e, cl=cl, h=h, B2=B2, part=part, r0=r0: e.scalar_tensor_tensor(
                                out=B2[:, cl, :].rearrange("p (a t) -> p a t", a=16),
                                in0=ps[7][:, r0:r0 + 128].rearrange("p (a t) -> p a t", a=16),
                                scalar=sgb_col(j, part * 6 + cl),
                                in1=bbp[:, h, 0:8].unsqueeze(1).to_broadcast([128, 16, 8]),
                                op0=ALU.mult, op1=ALU.add),
                                reads=[("ps7m", variant), "bbp", "colp", "BIGL"], writes=[("B2", variant)])
            if part == 0:
                issue_wu(0)
            for cl in range(6):
                cg = part * 6 + cl
                s = cg % 8
                for (g0, gs) in sgd.groups:
                    b = sg_ctr[0] % 2
                    sg_ctr[0] += 1
                    ub = 2 + b

                    def _m(e, s=s, g0=g0, gs=gs, ub=ub):
                        ins = None
                        for k in range(8):
                            ins = e.matmul(ps[ub][:, 0:gs], lhsT=uslots[:, s, k, :], rhs=xT[:, k, g0:g0 + gs],
                                           start=(k == 0), stop=(k == 7))
                        return ins
                    tiles = range(g0 // 128, (g0 + gs) // 128)
                    P.op("pe", _m, reads=["WRL", ("wu", s)] + [("xT", i) for i in tiles], writes=["ps%d" % ub],
                         cost=8 * gs / 1950.0)
                    P.op("act", lambda e, ub=ub, cl=cl, g0=g0, gs=gs: e.activation(
                        out=uT[:, cl, g0:g0 + gs], in_=ps[ub][:, 0:gs], func=AF.Gelu),
                        reads=["ps%d" % ub, "BIGL"], writes=[("uT", cl)])
            P.dma("pool", "wout", lambda e, part=part: e.dma_start(out=wout[:, :, :],
                                                                  in_=w_out[:, part * 6:(part + 1) * 6, :]),
                  reads=["BIGL"], writes=["swout"])
            if part < 3:
                issue_wu(part + 1)
            def emit_vn(i):
                mb = i % 2
                P.op("act", lambda e, i=i, mb=mb, part=part: e.activation(
                    out=vn[:, mb, :], in_=vraw[:, i, part * 768:(part + 1) * 768], func=AF.Identity,
                    bias=svst[:, i, 39:40], scale=svst[:, i, 38:39]),
                    reads=[("vraw", i), ("srstd", i), ("snmr", i), "BIGL", "PREPL"], writes=[("vn", mb)])

            def mix_gate(i):
                sample = (i >= sgd.NP)
                WT = WTs if sample else WTp
                B2 = B2s if sample else B2p
                mb = i % 2

                def mview(cl, mb=mb):
                    if cl < 4:
                        return ps[0 if mb == 0 else 1][:, cl * 128:(cl + 1) * 128]
                    q = (cl - 4) + 2 * mb
                    return ps[7][:, q * 128:(q + 1) * 128]
                mkeys = ["ps%d" % (0 if mb == 0 else 1), ("ps7m", mb)]

                def _mm(e, mb=mb, part=part, WT=WT, mview=mview):
                    ins = None
                    for cl in range(6):
                        h = part * 2 + cl // 3
                        ins = e.matmul(mview(cl), lhsT=vn[:, mb, cl * 128:(cl + 1) * 128], rhs=WT[:, h, :],
                                       start=True, stop=True)
                    return ins
                P.op("pe", _mm, reads=[("vn", mb), ("WT", sample), "BIGL"], writes=mkeys)
                for cl in range(6):
                    P.op("dve", lambda e, cl=cl, mb=mb, part=part, B2=B2, mview=mview: e.scalar_tensor_tensor(
                        out=mtmp[:, mb, cl, :], in0=mview(cl), scalar=sgg_col(j, part * 6 + cl), in1=B2[:, cl, :],
                        op0=ALU.mult, op1=ALU.add),
                        reads=mkeys + [("B2", int(sample)), "colp", "BIGL", "PREPL"], writes=[("mtmp", mb, cl)])
                P.op(GATE_ENG, lambda e, i=i, mb=mb: e.tensor_tensor(
                    out=uT[:, :, i * 128:(i + 1) * 128], in0=mtmp[:, mb, :, :], in1=uT[:, :, i * 128:(i + 1) * 128],
                    op=ALU.mult),
                    reads=[("mtmp", mb, cl) for cl in range(6)] + [("uT", cl) for cl in range(6)] + ["BIGL"],
                    writes=[("uTg", i)])
            def outp(i):
                out_proj(sgd, i, 6, lambda c, i: uT[:, c, i * 128:(i + 1) * 128], lambda c: wout[:, c, :],
                         [("uTg", i), "swout", "BIGL"] + [("uT", cl) for cl in range(6)])
                if part < 3:
                    sc = ALPHA if part == 0 else 1.0
                    for h, yb in enumerate((4, 5)):
                        P.op("dve", lambda e, i=i, h=h, yb=yb, sc=sc: e.scalar_tensor_tensor(
                            out=xres[:, i, h * 512:(h + 1) * 512], in0=xres[:, i, h * 512:(h + 1) * 512],
                            scalar=float(sc), in1=ps[yb][:, :], op0=ALU.mult, op1=ALU.add),
                            reads=[("xres", i), "ps%d" % yb], writes=[("xres", i)])
                else:
                    epilogue(sgd, i, 4, 5, 1.0, False)
            emit_vn(0)
            for i in range(NT):
                if i + 1 < NT:
                    emit_vn(i + 1)
                mix_gate(i)
                if i >= 1:
                    outp(i - 1)
            outp(NT - 1)
            flush_epi()
            if sgd.has_sample and part < 3:
                i = NT - 1
                for q in range(2):
                    f = 2 * part + q
                    P.op("act", lambda e, q=q, f=f, i=i: e.activation(
                        out=sgt[:, q, :], in_=vraw[:, i, f * 512:(f + 1) * 512], func=AF.Identity,
                        bias=svst[:, i, 39:40], scale=svst[:, i, 38:39]),
                        reads=[("vraw", i), ("srstd", i), ("snmr", i), "BIGL"], writes=[("sgt", q)])
                    P.op("dve", lambda e, q=q: e.tensor_tensor(out=sgt[:, q, :], in0=sgt[:, q, :],
                                                               in1=tbuf[:, 0, q * 512:(q + 1) * 512], op=ALU.mult),
                         reads=[("sgt", q), ("t", 0, q)], writes=[("sgt", q), ("t", 0, q)])
                    P.op("dve", lambda e, q=q: e.tensor_tensor(out=sgt[:, q, :], in0=sgt[:, q, :],
                                                               in1=tbuf[:, 1, q * 512:(q + 1) * 512], op=ALU.add),
                         reads=[("sgt", q), ("t", 1, q)], writes=[("sgt", q), ("t", 1, q)])
                    P.dma("sp", "svo%d" % q, lambda e, q=q, f=f: e.dma_start(out=st_sv[j, :, f * 512:(f + 1) * 512],
                                                                       in_=sgt[:, q, :]),
                          reads=[("sgt", q)])

    sgA = SG("A", list(range(0, 8)), False, [(0, 512), (512, 512)])
    sgB = SG("B", list(range(8, 16)), True, [(0, 384), (384, 384), (768, 384)])
    n_sub_total = DEPTH * 3
    for sgd in (sgA, sgB):
        for i in range(sgd.NT):
            if i < sgd.NP:
                g = sgd.ptiles[i]
                src = xp[g * 128:(g + 1) * 128, :]
            else:
                src = xs[:, :]
            P.dma("sp", "xin%d" % i, lambda e, i=i, src=src: e.dma_start(out=xres[:, i, :], in_=src),
                  writes=[("xres", i)])
            emit_xT(i)
        nsub = 0
        stop = False
        for l in range(DEPTH):
            for sub in range(3):
                if DEBUG_STOP is not None and nsub >= DEBUG_STOP:
                    stop = True
                    break
                is_last = (l == DEPTH - 1 and sub == 2)
                P.fence("BIGL")
                P.fence("WRL")
                if sub == 0:
                    ffn(sgd, l, 0, False)
                elif sub == 1:
                    if l % 2 == 0:
                        conv(sgd, l)
                    else:
                        sgu(sgd, l)
                else:
                    ffn(sgd, l, 1, is_last)
                nsub += 1
            if stop:
                break
        if stop:
            for i in range(sgd.NT):
                P.dma("sp", "yo%d" % i, lambda e, i=i, sgd=sgd: e.dma_start(out=out_rows(sgd, i), in_=xres[:, i, :]),
                      reads=[("xres", i)])
    P.flush()
    for sname in sorted(P.is_dma_sem):
        if sname.startswith(("yo", "svo", "sto")):
            P.final_wait("sp", sname)
    P.emit()
    stack.close()
    return nc


_CONSTS = None


def _consts():
    global _CONSTS
    if _CONSTS is None:
        t = np.arange(128)
        ident = np.eye(128, dtype=np.float32)
        tril = (t[None, :] <= t[:, None]).astype(np.float32)
        blk = ((t[None, :] // 8 == t[:, None] // 8) & (t[None, :] <= t[:, None])).astype(np.float32)
        _CONSTS = (ident, tril, blk)
    return _CONSTS


def kernel(x_prompt, x_sample, cache_conv, ln_g, ln_b, ffn1_w_in, ffn1_w_out, ffn2_w_in, ffn2_w_out,
           conv_w_in, conv_w, conv_w_out, sgu_w_in, sgu_ln_g, sgu_ln_b, sgu_w_s, sgu_b_s, sgu_w_out):
    f = lambda a: np.ascontiguousarray(np.asarray(a, dtype=np.float32))
    x_prompt, x_sample, cache_conv = f(x_prompt), f(x_sample), f(cache_conv)
    shared = {
        "ln_g": f(ln_g), "ln_b": f(ln_b), "ffn1_w_in": f(ffn1_w_in), "ffn1_w_out": f(ffn1_w_out),
        "ffn2_w_in": f(ffn2_w_in), "ffn2_w_out": f(ffn2_w_out), "conv_w_in": f(conv_w_in), "conv_w": f(conv_w),
        "conv_w_out": f(conv_w_out), "sgu_w_in": f(sgu_w_in), "sgu_ln_g": f(sgu_ln_g), "sgu_ln_b": f(sgu_ln_b),
        "sgu_w_s": f(sgu_w_s), "sgu_b_s": f(sgu_b_s), "sgu_w_out": f(sgu_w_out),
    }
    ident, tril, blk = _consts()
    shared.update({"c_ident": ident, "c_tril": tril, "c_blk": blk})
    in_maps = []
    for c in range(NCORES):
        m = dict(shared)
        m["xp"] = x_prompt[c]
        m["xs"] = np.ascontiguousarray(x_sample[16 * c:16 * (c + 1)].reshape(128, D))
        m["cache"] = np.ascontiguousarray(cache_conv[:, 16 * c:16 * (c + 1)].reshape(2, 32, D))
        in_maps.append(m)
    nc = build_nc()
    res = run_bass_kernel_spmd(nc, in_maps, core_ids=list(range(NCORES)))
    r = res.results
    y_p = np.stack([r[c]["y_p"] for c in range(NCORES)], axis=0).reshape(8, 2048, D)
    y_s = np.concatenate([r[c]["y_s"].reshape(16, 8, D) for c in range(NCORES)], axis=0)
    st_cp = np.stack([r[c]["st_cp"] for c in range(NCORES)], axis=1).reshape(2, 8, 2, D)
    st_cs = np.concatenate([r[c]["st_cs"].reshape(2, 16, 2, D) for c in range(NCORES)], axis=1)
    st_sv = np.concatenate([r[c]["st_sv"].reshape(2, 16, 8, DSGU) for c in range(NCORES)], axis=1)
    return (y_p.astype(np.float32), y_s.astype(np.float32), st_cp.astype(np.float32),
            st_cs.astype(np.float32), st_sv.astype(np.float32))
```

```python
import numpy as np
from contextlib import ExitStack
import concourse.bass as bass
import concourse.mybir as mybir
from concourse.bass_utils import run_bass_kernel_spmd

F32 = mybir.dt.float32
BF16 = mybir.dt.bfloat16
AF = mybir.ActivationFunctionType
ALU = mybir.AluOpType

D = 1024
DEPTH = 4
DFF = 2816
NJ = DFF // 128
DSGU = 3072
ALPHA = (2.0 * DEPTH) ** 0.25
EPS = 1e-5
NCORES = 8
MAXNT = 9
MAXTOK = MAXNT * 128

DEFER_N = 26.0
GATE_ENG = "pool"
DEBUG_STOP = None


class Prog:
    ENGS = ("pe", "act", "dve", "pool", "sp")

    def __init__(self, nc, stack):
        self.nc = nc
        self.stack = stack
        self.streams = {e: [] for e in self.ENGS}
        self.sems = {}
        self.cnt = {}
        self.is_dma_sem = set()
        self.last_w = {}
        self.readers = {}
        self.base = {}
        self.pending = []
        for e in self.ENGS:
            self._sem("E_" + e)

    def defer(self, countdown, write_keys, emit_fn):
        self.pending.append([countdown, set(write_keys), emit_fn])

    def flush(self, upto=None):
        n = len(self.pending) if upto is None else upto + 1
        todo, self.pending = self.pending[:n], self.pending[n:]
        for _, _, fn in todo:
            fn()

    def _check_pending(self, reads, writes):
        if not self.pending:
            return
        keys = set(reads) | set(writes)
        hit = -1
        for idx, (_, wk, _) in enumerate(self.pending):
            if wk & keys:
                hit = idx
        if hit >= 0:
            self.flush(hit)

    def _sem(self, name):
        if name not in self.sems:
            self.sems[name] = self.stack.enter_context(self.nc.semaphore(name))
            self.cnt[name] = 0
        return self.sems[name]

    def _deps(self, reads, writes):
        deps = {}

        def add(tok):
            if tok is None:
                return
            s, v = tok
            if deps.get(s, 0) < v:
                deps[s] = v

        for k in reads:
            add(self.last_w.get(k))
            for s, v in self.base.get(k, {}).items():
                add((s, v))
        for k in writes:
            add(self.last_w.get(k))
            for s, v in self.readers.get(k, {}).items():
                add((s, v))
            for s, v in self.base.get(k, {}).items():
                add((s, v))
        return deps

    def _commit(self, tok, reads, writes):
        s, v = tok
        for k in reads:
            r = self.readers.setdefault(k, {})
            if r.get(s, 0) < v:
                r[s] = v
        for k in writes:
            self.last_w[k] = tok
            self.readers[k] = {}
            self.base.pop(k, None)

    def fence(self, k):
        b = dict(self.base.get(k, {}))
        lw = self.last_w.pop(k, None)
        if lw is not None:
            s, v = lw
            b[s] = max(b.get(s, 0), v)
        for s, v in self.readers.pop(k, {}).items():
            b[s] = max(b.get(s, 0), v)
        self.base[k] = b

    def op(self, eng, fn, reads=(), writes=(), tick=True, cost=0.5):
        self._check_pending(reads, writes)
        deps = self._deps(reads, writes)
        name = "E_" + eng
        if eng == "pe":
            deps.pop(name, None)
        self.cnt[name] += 1
        tok = (name, self.cnt[name])
        self.streams[eng].append((deps, fn, (name, 1)))
        self._commit(tok, reads, writes)
        if eng == "pe" and tick and self.pending:
            for p in self.pending:
                p[0] -= cost
            while self.pending and self.pending[0][0] <= 0:
                self.flush(0)

    def dma(self, queue, sem, fn, reads=(), writes=()):
        self._check_pending(reads, writes)
        self._sem(sem)
        self.is_dma_sem.add(sem)
        deps = self._deps(reads, writes)
        self.cnt[sem] += 16
        tok = (sem, self.cnt[sem])
        self.streams[queue].append((deps, fn, (sem, 16)))
        self._commit(tok, reads, writes)

    def final_wait(self, queue, sem):
        self.streams[queue].append(({sem: self.cnt[sem]}, None, None))

    def emit(self):
        nc = self.nc
        with nc.Block() as block:
            def run(ename):
                def body(e):
                    waited = {}
                    for deps, fn, inc in self.streams[ename]:
                        for s, v in deps.items():
                            if waited.get(s, 0) >= v:
                                continue
                            e.wait_ge(self.sems[s], v)
                            waited[s] = v
                        if fn is None:
                            continue
                        ins = fn(e)
                        ins.then_inc(self.sems[inc[0]], inc[1])
                return body
            if self.streams["pe"]:
                block.tensor(run("pe"))
            if self.streams["act"]:
                block.scalar(run("act"))
            if self.streams["dve"]:
                block.vector(run("dve"))
            if self.streams["pool"]:
                block.gpsimd(run("pool"))
            if self.streams["sp"]:
                block.sync(run("sp"))


class SG:
    def __init__(self, name, ptiles, has_sample, groups):
        self.name = name
        self.ptiles = ptiles
        self.NP = len(ptiles)
        self.has_sample = has_sample
        self.NT = self.NP + (1 if has_sample else 0)
        self.TOK = self.NT * 128
        self.PL = self.NP * 128
        self.groups = groups
        assert sum(g[1] for g in groups) == self.TOK


def build_nc():
    nc = bass.Bass("TRN2", target_bir_lowering=False)
    stack = ExitStack()

    def din(name, shape):
        return nc.dram_tensor(name, list(shape), F32, kind="ExternalInput").ap()

    def dout(name, shape):
        return nc.dram_tensor(name, list(shape), F32, kind="ExternalOutput").ap()

    xp = din("xp", (2048, D))
    xs = din("xs", (128, D))
    cache = din("cache", (2, 32, D))
    ln_g = din("ln_g", (DEPTH, 3, D))
    ln_b = din("ln_b", (DEPTH, 3, D))
    f1_in = din("ffn1_w_in", (DEPTH, D, 2 * DFF))
    f1_out = din("ffn1_w_out", (DEPTH, DFF, D))
    f2_in = din("ffn2_w_in", (DEPTH, D, 2 * DFF))
    f2_out = din("ffn2_w_out", (DEPTH, DFF, D))
    cv_in = din("conv_w_in", (2, D, 3 * D))
    cv_w = din("conv_w", (2, 3, D))
    cv_out = din("conv_w_out", (2, D, D))
    sg_in = din("sgu_w_in", (2, D, 2 * DSGU))
    sg_g = din("sgu_ln_g", (2, DSGU))
    sg_b = din("sgu_ln_b", (2, DSGU))
    sg_ws = din("sgu_w_s", (2, 8, 128, 128))
    sg_bs = din("sgu_b_s", (2, 8, 128))
    sg_out = din("sgu_w_out", (2, DSGU, D))
    c_ident = din("c_ident", (128, 128))
    c_tril = din("c_tril", (128, 128))
    c_blk = din("c_blk", (128, 128))

    y_p = dout("y_p", (2048, D))
    y_s = dout("y_s", (128, D))
    st_cp = dout("st_cp", (2, 2, D))
    st_cs = dout("st_cs", (2, 32, D))
    st_sv = dout("st_sv", (2, 128, DSGU))

    def sb(name, shape, dt):
        return stack.enter_context(nc.sbuf_tensor(name, list(shape), dt))

    xres = sb("xres", (128, MAXNT, D), F32)
    xT = sb("xT", (128, 8, MAXTOK), BF16)
    BIG_F32 = 27712
    big = sb("big", (128, BIG_F32), F32)
    wring = sb("wring", (128, 8192), BF16)
    lnp = sb("lnp", (128, 2, D), F32)
    tbuf = sb("tbuf", (128, 2, D), F32)
    xbf = sb("xbf", (128, 2, D), BF16)
    sgt = sb("sgt", (128, 2, 512), F32)
    stt = sb("stt", (128, 2, 40), F32)
    ident_f = sb("ident_f", (128, 128), F32)
    ident_b = sb("ident_b", (128, 128), BF16)
    tril_f = sb("tril_f", (128, 128), F32)
    blk_f = sb("blk_f", (128, 128), F32)
    ones_b = sb("ones_b", (128, 128), BF16)
    eps_t = sb("eps_t", (128, 8), F32)
    colp = sb("colp", (128, 144), F32)
    pnat = sb("pnat", (128, 2, 128), F32)
    hist = sb("hist", (128, 2, 8, 2), F32)
    ps = [stack.enter_context(nc.psum_tensor("ps%d" % i, [128, 512], F32)) for i in range(8)]

    P = Prog(nc, stack)
    PS7 = ["ps7"]

    def pk(q):
        return list(PS7) if q == 7 else ["ps%d" % q]

    def bigv(off_f32, n_f32, dt=F32):
        v = big[:, off_f32:off_f32 + n_f32]
        if dt == BF16:
            v = v.bitcast(BF16)
        return v

    P.dma("sp", "c0", lambda e: e.dma_start(out=ident_f[:], in_=c_ident[:, :]), writes=["ident_f"])
    P.dma("sp", "c1", lambda e: e.dma_start(out=tril_f[:], in_=c_tril[:, :]), writes=["tril_f"])
    P.dma("sp", "c2", lambda e: e.dma_start(out=blk_f[:], in_=c_blk[:, :]), writes=["blk_f"])
    P.op("act", lambda e: e.copy(out=ident_b[:], in_=ident_f[:]), reads=["ident_f"], writes=["ident_b"])
    P.op("dve", lambda e: e.memset(ones_b[:], 1.0), writes=["ones_b"])
    P.op("dve", lambda e: e.memset(eps_t[:], EPS), writes=["eps_t"])
    P.dma("sp", "c3", lambda e: e.dma_start(
        out=pnat[0:48, 0, :], in_=cv_w.rearrange("j k (c p) -> (j k c) p", p=128)), writes=["pnat0"])
    P.dma("sp", "c3", lambda e: e.dma_start(
        out=pnat[48:96, 0, :], in_=sg_g.rearrange("j (c p) -> (j c) p", p=128)), writes=["pnat1"])
    P.dma("sp", "c3", lambda e: e.dma_start(
        out=pnat[0:48, 1, :], in_=sg_b.rearrange("j (c p) -> (j c) p", p=128)), writes=["pnat2"])

    def _tp(e):
        e.transpose(out=ps[6][:, 0:96], in_=pnat[0:96, 0, :], identity=ident_f[0:96, 0:96])
        return e.transpose(out=ps[6][:, 96:144], in_=pnat[0:48, 1, :], identity=ident_f[0:48, 0:48])
    P.op("pe", _tp, reads=["pnat0", "pnat1", "pnat2", "ident_f"], writes=["ps6"])
    P.op("act", lambda e: e.copy(out=colp[:, :], in_=ps[6][:, 0:144]), reads=["ps6"], writes=["colp"])

    def cw_col(j, k, c):
        i = (j * 3 + k) * 8 + c
        return colp[:, i:i + 1]

    def sgg_col(j, c):
        i = 48 + j * 24 + c
        return colp[:, i:i + 1]

    def sgb_col(j, c):
        i = 96 + j * 24 + c
        return colp[:, i:i + 1]

    xbf_ctr = [0]
    t_ctr = [0]
    sg_ctr = [0]

    def emit_xT(i, defer=True):
        b = xbf_ctr[0] % 2
        xbf_ctr[0] += 1
        P.op("act", lambda e: e.copy(out=xbf[:, b, :], in_=xres[:, i, :]),
             reads=[("xres", i)], writes=[("xbf", b)])
        tbank = 6 if b == 0 else 7
        tb = ps[tbank][:, :].bitcast(BF16)

        def _t(e):
            ins = None
            for k in range(8):
                ins = e.transpose(out=tb[:, k * 128:(k + 1) * 128], in_=xbf[:, b, k * 128:(k + 1) * 128],
                                  identity=ident_b[:, :])
            return ins

        def emit_tail():
            P.op("pe", _t, reads=[("xbf", b), "ident_b"], writes=pk(tbank), tick=False)
            P.op("act", lambda e: e.copy(out=xT[:, :, i * 128:(i + 1) * 128],
                                         in_=tb.rearrange("p (k t) -> p k t", k=8)),
                 reads=pk(tbank), writes=[("xT", i)])
        if defer:
            P.defer(DEFER_N, [("xT", i), ("xbf", b)] + pk(tbank), emit_tail)
        else:
            emit_tail()

    def load_lnp(l, s):
        P.dma("sp", "lnp", lambda e: e.dma_start(out=lnp[:, 0, :], in_=ln_g[l, s, :].partition_broadcast(128)),
              writes=["lnp0"])
        P.dma("sp", "lnp", lambda e: e.dma_start(out=lnp[:, 1, :], in_=ln_b[l, s, :].partition_broadcast(128)),
              writes=["lnp1"])

    def out_rows(sgd, i):
        if i < sgd.NP:
            g = sgd.ptiles[i]
            return y_p[g * 128:(g + 1) * 128, :]
        return y_s[:, :]

    epi_pending = []

    def flush_epi():
        while epi_pending:
            epi_pending.pop(0)()

    def epilogue(sgd, i, y0, y1, x_scale, is_last, in_xres=None):
        b = t_ctr[0] % 2
        t_ctr[0] += 1
        st = stt[:, b, :]
        for h, yb in enumerate((y0, y1)):
            P.op("dve", lambda e, h=h, yb=yb: e.scalar_tensor_tensor(
                out=tbuf[:, b, h * 512:(h + 1) * 512], in0=xres[:, i, h * 512:(h + 1) * 512], scalar=float(x_scale),
                in1=ps[yb][:, :], op0=ALU.mult, op1=ALU.add),
                reads=[("xres", i), "ps%d" % yb], writes=[("t", b, h)])
        for h in range(2):
            P.op("dve", lambda e, h=h: e.bn_stats(out=st[:, h * 6:(h + 1) * 6], in_=tbuf[:, b, h * 512:(h + 1) * 512]),
                 reads=[("t", b, h)], writes=[("st", b, h)])
        P.op("dve", lambda e: e.bn_aggr(out=st[:, 12:14], in_=st[:, 0:12].rearrange("p (n t) -> p n t", t=3)),
             reads=[("st", b, 0), ("st", b, 1)], writes=[("mv", b)])
        P.op("act", lambda e: e.activation(out=st[:, 16:17], in_=st[:, 13:14], func=AF.Ln, bias=eps_t[:, 0:1], scale=1.0),
             reads=[("mv", b), "eps_t"], writes=[("lnv", b)])
        P.op("act", lambda e: e.activation(out=st[:, 14:15], in_=st[:, 16:17], func=AF.Exp, scale=-0.5),
             reads=[("lnv", b)], writes=[("rstd", b)])
        P.op("dve", lambda e: e.tensor_scalar(out=st[:, 17:18], in0=st[:, 12:13], scalar1=-1.0, scalar2=None,
                                              op0=ALU.mult),
             reads=[("mv", b)], writes=[("negm", b)])
        P.op("act", lambda e: e.activation(out=st[:, 15:16], in_=st[:, 14:15], func=AF.Identity, scale=st[:, 17:18]),
             reads=[("rstd", b), ("negm", b)], writes=[("nmr", b)])
        P.op("act", lambda e: e.activation(out=tbuf[:, b, :], in_=tbuf[:, b, :], func=AF.Identity,
                                           bias=st[:, 15:16], scale=st[:, 14:15]),
             reads=[("t", b, 0), ("t", b, 1), ("rstd", b), ("nmr", b)], writes=[("t", b, 0), ("t", b, 1)])

        def stage_b():
            P.op("dve", lambda e: e.tensor_tensor(out=tbuf[:, b, :], in0=tbuf[:, b, :], in1=lnp[:, 0, :], op=ALU.mult),
                 reads=[("t", b, 0), ("t", b, 1), "lnp0", "lnp1"], writes=[("t", b, 0), ("t", b, 1), "lnp0", "lnp1"])
            P.op("dve", lambda e: e.tensor_tensor(out=xres[:, i, :], in0=tbuf[:, b, :], in1=lnp[:, 1, :], op=ALU.add),
                 reads=[("t", b, 0), ("t", b, 1), "lnp0", "lnp1"], writes=[("xres", i), "lnp0", "lnp1"])
            if is_last:
                P.dma("sp", "yo%d" % i, lambda e: e.dma_start(out=out_rows(sgd, i), in_=xres[:, i, :]),
                      reads=[("xres", i)])
            else:
                emit_xT(i)
        flush_epi()
        epi_pending.append(stage_b)

    def out_proj(sgd, i, n_chunks, lhs_fn, rhs_fn, key_reads, key_writes=()):
        for n, yb in enumerate((4, 5)):
            def _m(e, n=n, yb=yb):
                ins = None
                for c in range(n_chunks):
                    ins = e.matmul(ps[yb][:, :], lhsT=lhs_fn(c, i), rhs=rhs_fn(c)[:, n * 512:(n + 1) * 512],
                                   start=(c == 0), stop=(c == n_chunks - 1))
                return ins
            P.op("pe", _m, reads=key_reads, writes=["ps%d" % yb] + list(key_writes), cost=n_chunks * 512 / 1950.0)

    def ffn(sgd, l, which, is_last):
        w_in = (f1_in, f2_in)[which][l].rearrange("(k p) n -> p k n", p=128)
        w_out = (f1_out, f2_out)[which][l].rearrange("(j p) n -> p j n", p=128)
        TOK = sgd.TOK
        hT = bigv(0, NJ * MAXTOK // 2, BF16).rearrange("p (j t) -> p j t", j=NJ)
        wo_off = NJ * MAXTOK // 2
        wout = bigv(wo_off, NJ * D // 2, BF16).rearrange("p (j n) -> p j n", j=NJ)
        assert wo_off + NJ * D // 2 <= BIG_F32
        ws4 = wring[:, :].rearrange("p (s k n) -> p s k n", s=4, k=8)
        sp_off = wo_off + NJ * D // 2
        ws3 = bigv(sp_off, 3 * 1024, BF16).rearrange("p (s k n) -> p s k n", s=3, k=8)
        assert sp_off + 3 * 1024 <= BIG_F32
        NS = 7
        JB = 4 if len(sgd.groups) == 2 else 3

        def wsl(sl):
            return ws4[:, sl] if sl < 4 else ws3[:, sl - 4]
        wo_piece = [0]

        def issue_wout():
            j0 = wo_piece[0]
            if j0 >= NJ:
                return
            j1 = min(NJ, j0 + 2)
            wo_piece[0] = j1
            P.dma("pool", "wout", lambda e: e.dma_start(out=wout[:, j0:j1, :], in_=w_out[:, j0:j1, :]),
                  reads=["BIGL"], writes=[("wout", j) for j in range(j0, j1)])

        for j0 in range(0, NJ, JB):
            blk = list(range(j0, min(NJ, j0 + JB)))
            for j in blk:
                sl = j % NS
                P.dma("pool", "wr%d" % sl, lambda e, sl=sl, j=j: e.dma_start(
                    out=wsl(sl)[:, :, 0:128], in_=w_in[:, :, j * 128:(j + 1) * 128]),
                    reads=["WRL"] + (["BIGL"] if sl >= 4 else []), writes=[("wr", sl, 0)])
                P.dma("pool", "wr%d" % sl, lambda e, sl=sl, j=j: e.dma_start(
                    out=wsl(sl)[:, :, 128:256], in_=w_in[:, :, DFF + j * 128:DFF + (j + 1) * 128]),
                    reads=["WRL"] + (["BIGL"] if sl >= 4 else []), writes=[("wr", sl, 1)])
                if j >= NS - 1:
                    issue_wout()
            for (g0, gs) in sgd.groups:
                for j in blk:
                    sl = j % NS
                    b = sg_ctr[0] % 2
                    sg_ctr[0] += 1
                    gb, ub = (0, 2) if b == 0 else (1, 3)

                    def _m(e, sl=sl, g0=g0, gs=gs, gb=gb, ub=ub):
                        ins = None
                        for half, bank in ((0, gb), (1, ub)):
                            for k in range(8):
                                ins = e.matmul(ps[bank][:, 0:gs], lhsT=wsl(sl)[:, k, half * 128:(half + 1) * 128],
                                               rhs=xT[:, k, g0:g0 + gs], start=(k == 0), stop=(k == 7))
                        return ins
                    tiles = range(g0 // 128, (g0 + gs) // 128)
                    P.op("pe", _m, reads=["WRL", "BIGL", ("wr", sl, 0), ("wr", sl, 1)] + [("xT", i) for i in tiles],
                         writes=["ps%d" % gb, "ps%d" % ub, ("wr", sl, 0), ("wr", sl, 1)], cost=16 * gs / 1950.0)
                    P.op("act", lambda e, b=b, gb=gb, gs=gs: e.activation(out=sgt[:, b, 0:gs], in_=ps[gb][:, 0:gs],
                                                                           func=AF.Silu),
                         reads=["ps%d" % gb], writes=[("sgt", b)])
                    P.op("dve", lambda e, b=b, ub=ub, j=j, g0=g0, gs=gs: e.scalar_tensor_tensor(
                        out=hT[:, j, g0:g0 + gs], in0=sgt[:, b, 0:gs], scalar=0.5, in1=ps[ub][:, 0:gs],
                        op0=ALU.mult, op1=ALU.mult),
                        reads=[("sgt", b), "ps%d" % ub, "BIGL"], writes=[("hT", j)])
        while wo_piece[0] < NJ:
            issue_wout()
        load_lnp(l, 0 if which == 0 else 2)
        for i in range(sgd.NT):
            out_proj(sgd, i, NJ, lambda c, i: hT[:, c, i * 128:(i + 1) * 128], lambda c: wout[:, c, :],
                     [("hT", j) for j in range(NJ)] + [("wout", j) for j in range(NJ)] + ["BIGL"],
                     [("wout", j) for j in range(NJ)])
            epilogue(sgd, i, 4, 5, ALPHA, is_last)
        flush_epi()

    def conv(sgd, l):
        j = l // 2
        w_in = cv_in[j].rearrange("(k p) n -> p k n", p=128)
        w_out = cv_out[j].rearrange("(c p) n -> p c n", p=128)
        TOK, PL = sgd.TOK, sgd.PL
        off = 0
        yT = bigv(off, 8 * MAXTOK // 2, BF16).rearrange("p (c t) -> p c t", c=8); off += 8 * MAXTOK // 2
        wout = bigv(off, 8 * D // 2, BF16).rearrange("p (c n) -> p c n", c=8); off += 8 * D // 2
        gbuf = bigv(off, 2 * 1032).rearrange("p (b t) -> p b t", b=2); off += 2 * 1032
        gsb = bigv(off, 2 * 160).rearrange("p (b a t) -> p b a t", b=2, a=16); off += 2 * 160
        bT = bigv(off, 2 * MAXTOK).rearrange("p (b t) -> p b t", b=2); off += 2 * MAXTOK
        cvb = bigv(off, MAXTOK); off += MAXTOK
        stT = bigv(off, 8 * 34).rearrange("p (c r) -> p c r", c=8); off += 8 * 34
        cacheT = bigv(off, 8 * 32).rearrange("p (c r) -> p c r", c=8); off += 8 * 32
        cnat = bigv(off, D); off += D
        stout = bigv(off, D); off += D
        assert off <= BIG_F32
        wslots = wring[:, 0:2 * 8 * 384].rearrange("p (s k n) -> p s k n", s=2, k=8)
        load_lnp(l, 1)
        def issue_cwout():
            P.dma("pool", "wout", lambda e: e.dma_start(out=wout[:, :, :], in_=w_out[:, :, :]),
                  reads=["BIGL"], writes=["cwout"])
        if sgd.has_sample:
            P.dma("sp", "cnat", lambda e: e.dma_start(out=cnat[0:32, :], in_=cache[j, :, :]),
                  reads=["BIGL"], writes=["cnat"])

            def _tc(e):
                ins = None
                for c in range(8):
                    ins = e.transpose(out=ps[7][:, c * 32:(c + 1) * 32], in_=cnat[0:32, c * 128:(c + 1) * 128],
                                      identity=ident_f[0:32, 0:32])
                return ins
            P.op("pe", _tc, reads=["cnat", "ident_f", "BIGL"], writes=PS7)
            P.op("act", lambda e: e.copy(out=cacheT[:, :, :], in_=ps[7][:, 0:256].rearrange("p (c r) -> p c r", c=8)),
                 reads=PS7 + ["BIGL"], writes=["cacheT"])
        for c in range(8):
            s = c % 2
            cb = c % 2
            for part in range(3):
                P.dma("pool", "wr%d" % s, lambda e, s=s, c=c, part=part: e.dma_start(
                    out=wslots[:, s, :, part * 128:(part + 1) * 128],
                    in_=w_in[:, :, part * D + c * 128:part * D + (c + 1) * 128]), reads=["WRL"], writes=[("wr", s, part)])
            if c == 1:
                issue_cwout()
            for (g0, gs) in sgd.groups:
                b = sg_ctr[0] % 2
                sg_ctr[0] += 1
                banks = (0, 1, 2) if b == 0 else (3, 7, 6)

                def _m(e, s=s, g0=g0, gs=gs, banks=banks):
                    ins = None
                    for part in range(3):
                        for k in range(8):
                            ins = e.matmul(ps[banks[part]][:, 0:gs], lhsT=wslots[:, s, k, part * 128:(part + 1) * 128],
                                           rhs=xT[:, k, g0:g0 + gs], start=(k == 0), stop=(k == 7))
                    return ins
                tiles = range(g0 // 128, (g0 + gs) // 128)
                P.op("pe", _m, reads=["WRL", ("wr", s, 0), ("wr", s, 1), ("wr", s, 2)] + [("xT", i) for i in tiles],
                     writes=sum([pk(q) for q in banks], []) + [("wr", s, 0), ("wr", s, 1), ("wr", s, 2)],
                     cost=24 * gs / 1950.0)
                bb, cbk, hb = banks
                P.op("act", lambda e, b=b, cbk=cbk, gs=gs: e.copy(out=sgt[:, b, 0:gs], in_=ps[cbk][:, 0:gs]),
                     reads=pk(cbk), writes=[("sgt", b)])
                p1 = min(g0 + gs, PL)
                if g0 < p1:
                    n = p1 - g0
                    P.op("dve", lambda e, b=b, hb=hb, cb=cb, g0=g0, n=n: e.tensor_tensor(
                        out=gbuf[:, cb, 2 + g0:2 + g0 + n], in0=sgt[:, b, 0:n], in1=ps[hb][:, 0:n], op=ALU.mult),
                        reads=[("sgt", b), "BIGL"] + pk(hb), writes=[("gbuf", cb)])
                if g0 + gs > PL:
                    o = PL - g0
                    P.op("dve", lambda e, b=b, hb=hb, cb=cb, o=o: e.tensor_tensor(
                        out=gsb[:, cb, :, 2:10], in0=sgt[:, b, o:o + 128].rearrange("p (a t) -> p a t", a=16),
                        in1=ps[hb][:, o:o + 128].rearrange("p (a t) -> p a t", a=16), op=ALU.mult),
                        reads=[("sgt", b), "BIGL"] + pk(hb), writes=[("gsb", cb)])
                P.op("act", lambda e, bb=bb, cb=cb, g0=g0, gs=gs: e.copy(out=bT[:, cb, g0:g0 + gs], in_=ps[bb][:, 0:gs]),
                     reads=pk(bb) + ["BIGL"], writes=[("bT", cb)])
            if sgd.name == "A":
                P.op("dve", lambda e, cb=cb: e.memset(gbuf[:, cb, 0:2], 0.0), reads=["BIGL"], writes=[("gbuf", cb)])
            else:
                P.op("dve", lambda e, cb=cb, c=c: e.tensor_copy(out=gbuf[:, cb, 0:2], in_=hist[:, j, c, :]),
                     reads=[("hist", j, c), "BIGL"], writes=[("gbuf", cb)])
            if sgd.has_sample:
                P.op("dve", lambda e, cb=cb, c=c: e.tensor_copy(
                    out=gsb[:, cb, :, 0:2], in_=cacheT[:, c, :].rearrange("p (a r) -> p a r", a=16)),
                    reads=["cacheT", "BIGL"], writes=[("gsb", cb)])
            rk = [("gbuf", cb), ("gsb", cb), "colp", "BIGL"]
            P.op("dve", lambda e, cb=cb, c=c: e.tensor_scalar(out=cvb[:, 0:PL], in0=gbuf[:, cb, 0:PL],
                                                             scalar1=cw_col(j, 0, c), scalar2=None, op0=ALU.mult),
                 reads=rk, writes=["cvb"])
            for k in (1, 2):
                P.op("dve", lambda e, cb=cb, c=c, k=k: e.scalar_tensor_tensor(
                    out=cvb[:, 0:PL], in0=gbuf[:, cb, k:k + PL], scalar=cw_col(j, k, c), in1=cvb[:, 0:PL],
                    op0=ALU.mult, op1=ALU.add), reads=rk + ["cvb"], writes=["cvb"])
            if sgd.has_sample:
                cvs = cvb[:, PL:PL + 128].rearrange("p (a t) -> p a t", a=16)
                P.op("dve", lambda e, cb=cb, c=c: e.tensor_scalar(out=cvs, in0=gsb[:, cb, :, 0:8],
                                                                 scalar1=cw_col(j, 0, c), scalar2=None, op0=ALU.mult),
                     reads=rk, writes=["cvs"])
                for k in (1, 2):
                    P.op("dve", lambda e, cb=cb, c=c, k=k: e.scalar_tensor_tensor(
                        out=cvs, in0=gsb[:, cb, :, k:k + 8], scalar=cw_col(j, k, c), in1=cvs,
                        op0=ALU.mult, op1=ALU.add), reads=rk + ["cvs"], writes=["cvs"])
            P.op("dve", lambda e, cb=cb, c=c: e.tensor_tensor(out=yT[:, c, 0:TOK], in0=cvb[:, 0:TOK],
                                                             in1=bT[:, cb, 0:TOK], op=ALU.mult),
                 reads=["cvb", "cvs", ("bT", cb), "BIGL"], writes=[("yT", c)])
            if sgd.name == "A":
                P.op("dve", lambda e, cb=cb, c=c: e.tensor_copy(out=hist[:, j, c, :], in_=gbuf[:, cb, PL:PL + 2]),
                     reads=[("gbuf", cb), "BIGL"], writes=[("hist", j, c)])
            else:
                P.op("dve", lambda e, cb=cb, c=c: e.tensor_copy(
                    out=stT[:, c, 0:32].rearrange("p (a r) -> p a r", a=16), in_=gsb[:, cb, :, 8:10]),
                    reads=[("gsb", cb), "BIGL"], writes=[("stT", c)])
                P.op("dve", lambda e, cb=cb, c=c: e.tensor_copy(out=stT[:, c, 32:34], in_=gbuf[:, cb, PL:PL + 2]),
                     reads=[("gbuf", cb), "BIGL"], writes=[("stT", c)])
        if sgd.name == "B":
            for half, bank in ((0, 0), (1, 1)):
                def _ts(e, half=half, bank=bank):
                    ins = None
                    for q in range(4):
                        c = half * 4 + q
                        ins = e.transpose(out=ps[bank][0:34, q * 128:(q + 1) * 128], in_=stT[:, c, :],
                                          identity=ident_f[:, :])
                    return ins
                P.op("pe", _ts, reads=[("stT", c) for c in range(8)] + ["ident_f", "BIGL"], writes=["ps%d" % bank])
                P.op("act", lambda e, half=half, bank=bank: e.copy(out=stout[0:34, half * 512:(half + 1) * 512],
                                                                   in_=ps[bank][0:34, :]),
                     reads=["ps%d" % bank, "BIGL"], writes=[("stout", half)])
            P.dma("sp", "sto", lambda e: e.dma_start(out=st_cs[j, :, :], in_=stout[0:32, :]),
                  reads=[("stout", 0), ("stout", 1), "BIGL"])
            P.dma("sp", "sto", lambda e: e.dma_start(out=st_cp[j, :, :], in_=stout[32:34, :]),
                  reads=[("stout", 0), ("stout", 1), "BIGL"])
        for i in range(sgd.NT):
            out_proj(sgd, i, 8, lambda c, i: yT[:, c, i * 128:(i + 1) * 128], lambda c: wout[:, c, :],
                     [("yT", c) for c in range(8)] + ["cwout", "BIGL"])
            epilogue(sgd, i, 4, 5, ALPHA, False)
        flush_epi()

    def sgu(sgd, l):
        j = l // 2
        w_in = sg_in[j].rearrange("(k p) n -> p k n", p=128)
        w_out = sg_out[j].rearrange("(c p) n -> p c n", p=128)
        TOK, PL, NT = sgd.TOK, sgd.PL, sgd.NT
        off = 0
        vraw = bigv(off, MAXNT * DSGU // 2, BF16).rearrange("p (i f) -> p i f", i=MAXNT); off += MAXNT * DSGU // 2
        uT = bigv(off, 6 * MAXTOK // 2, BF16).rearrange("p (c t) -> p c t", c=6); off += 6 * MAXTOK // 2
        wout = bigv(off, 6 * D // 2, BF16).rearrange("p (c n) -> p c n", c=6); off += 6 * D // 2
        WTp = bigv(off, 8 * 128 // 2, BF16).rearrange("p (h t) -> p h t", h=8); off += 8 * 128 // 2
        WTs = bigv(off, 8 * 128 // 2, BF16).rearrange("p (h t) -> p h t", h=8); off += 8 * 128 // 2
        bbp = bigv(off, 8 * 128).rearrange("p (h t) -> p h t", h=8); off += 8 * 128
        B2p = bigv(off, 6 * 128).rearrange("p (c t) -> p c t", c=6); off += 6 * 128
        B2s = bigv(off, 6 * 128).rearrange("p (c t) -> p c t", c=6); off += 6 * 128
        wnb = bigv(off, 8 * 128 // 2, BF16).rearrange("p (h t) -> p h t", h=8)
        vn = bigv(off, 2 * 768 // 2, BF16).rearrange("p (b f) -> p b f", b=2); off += 2 * 768 // 2
        wnat = bigv(off, 8 * 128).rearrange("p (h t) -> p h t", h=8)
        mtmp = bigv(off, 2 * 768).rearrange("p (b c t) -> p b c t", b=2, c=6); off += 2 * 768
        svst = bigv(off, MAXNT * 48).rearrange("p (i q) -> p i q", i=MAXNT); off += MAXNT * 48
        assert off <= BIG_F32, off

        def prep_head(mask, sample):
            if not sample:
                P.dma("sp", "wnat", lambda e: e.dma_start(out=wnat[:, :, :], in_=sg_ws[j].rearrange("h t s -> t h s")),
                      reads=["BIGL", "PREPL"], writes=["wnat"] + [("wnatd", a) for a in range(16)])
            else:
                P.op("dve", lambda e: e.memset(wnat[:, :, :], 0.0), reads=["BIGL", "PREPL"],
                     writes=["wnat"] + [("wnatd", a) for a in range(16)])
                for a in range(16):
                    P.dma("sp", "wnat", lambda e, a=a: e.dma_start(
                        out=wnat[a * 8:(a + 1) * 8, :, a * 8:(a + 1) * 8],
                        in_=sg_ws[j, :, 0:8, 0:8].rearrange("h t s -> t h s")), reads=["PREPL", "BIGL", "wnat"],
                        writes=[("wnatd", a)])
            P.op("dve", lambda e: e.tensor_tensor(out=wnb[:, :, :], in0=wnat[:, :, :],
                                                  in1=mask[:, :].unsqueeze(1).to_broadcast([128, 8, 128]), op=ALU.mult),
                 reads=["wnat", "tril_f", "blk_f", "BIGL", "PREPL"] + [("wnatd", a) for a in range(16)],
                 writes=["wnb", "wnat"] + [("wnatd", a) for a in range(16)])

        def prep_tail(dst, sample):
            tb = ps[7][:, :].bitcast(BF16)

            def _t(e):
                ins = None
                for h in range(8):
                    ins = e.transpose(out=tb[:, h * 128:(h + 1) * 128], in_=wnb[:, h, :], identity=ident_b[:, :])
                return ins
            P.op("pe", _t, reads=["wnb", "ident_b", "PREPL", "BIGL"], writes=PS7 + ["wnb"])
            P.op("act", lambda e: e.copy(out=dst[:, :, :], in_=tb.rearrange("p (h t) -> p h t", h=8)),
                 reads=PS7 + ["BIGL"], writes=[("WT", sample)])

        prep_head(tril_f, False)
        P.dma("sp", "bb", lambda e: e.dma_start(
            out=bbp[:, :, :].rearrange("p h t -> p (h t)"),
            in_=sg_bs[j].rearrange("h t -> (h t)").partition_broadcast(128)), reads=["BIGL"], writes=["bbp"])

        vslots = wring[:, :].rearrange("p (s k n) -> p s k n", s=2, k=8)
        for f in range(6):
            s = f % 2
            P.dma("pool", "wr%d" % s, lambda e, s=s, f=f: e.dma_start(
                out=vslots[:, s, :, :], in_=w_in[:, :, DSGU + f * 512:DSGU + (f + 1) * 512]), reads=["WRL"], writes=[("wr", s, 0)])
            for i in range(NT):
                b = sg_ctr[0] % 2
                sg_ctr[0] += 1
                vb = b

                def _m(e, s=s, i=i, vb=vb):
                    ins = None
                    for k in range(8):
                        ins = e.matmul(ps[vb][:, :], lhsT=xT[:, k, i * 128:(i + 1) * 128], rhs=vslots[:, s, k, :],
                                       start=(k == 0), stop=(k == 7))
                    return ins
                P.op("pe", _m, reads=["WRL", ("wr", s, 0), ("xT", i)], writes=["ps%d" % vb], cost=8 * 512 / 1950.0)
                P.op("act", lambda e, b=b, vb=vb: e.activation(out=sgt[:, b, :], in_=ps[vb][:, :], func=AF.Gelu),
                     reads=["ps%d" % vb], writes=[("sgt", b)])
                P.op("dve", lambda e, b=b, i=i, f=f: e.bn_stats(out=svst[:, i, f * 6:(f + 1) * 6], in_=sgt[:, b, :]),
                     reads=[("sgt", b), "BIGL"], writes=[("svst", i, f)])
                P.op("act", lambda e, b=b, i=i, f=f: e.copy(out=vraw[:, i, f * 512:(f + 1) * 512], in_=sgt[:, b, :]),
                     reads=[("sgt", b), "BIGL"], writes=[("vraw", i)])
            if f == 0:
                prep_tail(WTp, False)
                if sgd.has_sample:
                    prep_head(blk_f, True)
            if f == 2 and sgd.has_sample:
                prep_tail(WTs, True)
        for i in range(NT):
            P.op("dve", lambda e, i=i: e.bn_aggr(out=svst[:, i, 36:38],
                                                 in_=svst[:, i, 0:36].rearrange("p (n t) -> p n t", t=3)),
                 reads=[("svst", i, f) for f in range(6)] + ["BIGL"], writes=[("smv", i)])
            P.op("act", lambda e, i=i: e.activation(out=svst[:, i, 40:41], in_=svst[:, i, 37:38], func=AF.Ln,
                                                    bias=eps_t[:, 0:1], scale=1.0),
                 reads=[("smv", i), "eps_t", "BIGL"], writes=[("slnv", i)])
            P.op("act", lambda e, i=i: e.activation(out=svst[:, i, 38:39], in_=svst[:, i, 40:41], func=AF.Exp, scale=-0.5),
                 reads=[("slnv", i), "BIGL"], writes=[("srstd", i)])
            P.op("dve", lambda e, i=i: e.scalar_tensor_tensor(out=svst[:, i, 39:40], in0=svst[:, i, 36:37], scalar=-1.0,
                                                              in1=svst[:, i, 38:39], op0=ALU.mult, op1=ALU.mult),
                 reads=[("smv", i), ("srstd", i), "BIGL"], writes=[("snmr", i)])

        load_lnp(l, 1)
        P.fence("PREPL")

        P.fence("WRL")
        uslots = wring[:, :].rearrange("p (s k n) -> p s k n", s=8, k=8)
        m_ctr = [0]
        MBANKS = (0, 1, 7)

        def issue_wu(part):
            for cl in range(6):
                cg = part * 6 + cl
                s = cg % 8
                P.dma("pool", "wu%d" % s, lambda e, s=s, cg=cg: e.dma_start(
                    out=uslots[:, s, :, :], in_=w_in[:, :, cg * 128:(cg + 1) * 128]), reads=["WRL"], writes=[("wu", s)])
        for part in range(4):
            if sgd.has_sample and part < 3:
                for q in range(2):
                    f = 2 * part + q
                    P.dma("sp", "gb%d" % q, lambda e, q=q, f=f: e.dma_start(
                        out=tbuf[:, 0, q * 512:(q + 1) * 512], in_=sg_g[j, f * 512:(f + 1) * 512].partition_broadcast(128)),
                        writes=[("t", 0, q)])
                    P.dma("sp", "gb%d" % (2 + q), lambda e, q=q, f=f: e.dma_start(
                        out=tbuf[:, 1, q * 512:(q + 1) * 512], in_=sg_b[j, f * 512:(f + 1) * 512].partition_broadcast(128)),
                        writes=[("t", 1, q)])
            variants = [(0, WTp, B2p)] + ([(1, WTs, B2s)] if sgd.has_sample else [])
            rs = {}
            for variant, WT, B2 in variants:
                for hh in range(2):
                    h = part * 2 + hh
                    bk = MBANKS[m_ctr[0] % 3]
                    m_ctr[0] += 1
                    rs[(variant, hh)] = bk
                    P.op("pe", lambda e, WT=WT, h=h, bk=bk: e.matmul(ps[bk][:, 0:128], lhsT=ones_b[:, :],
                                                                    rhs=WT[:, h, :], start=True, stop=True),
                         reads=[("WT", variant == 1), "ones_b", "BIGL"], writes=pk(bk))
            for variant, WT, B2 in variants:
                for hh in range(2):
                    h = part * 2 + hh
                    bk = rs[(variant, hh)]
                    for q in range(3):
                        cl = hh * 3 + q
                        if variant == 0:
                            P.op("dve", lambda e, cl=cl, h=h, B2=B2, part=part, bk=bk: e.scalar_tensor_tensor(
                                out=B2[:, cl, :], in0=ps[bk][:, 0:128], scalar=sgb_col(j, part * 6 + cl),
                                in1=bbp[:, h, :], op0=ALU.mult, op1=ALU.add),
                                reads=pk(bk) + ["bbp", "colp", "BIGL"], writes=[("B2", variant)])
                        else:
                            P.op("dve", lambda e, cl=cl, h=h, B2=B2, part=part, bk=bk: e.scalar_tensor_tensor(
                                out=B2[:, cl, :].rearrange("p (a t) -> p a t", a=16),
                                in0=ps[bk][:, 0:128].rearrange("p (a t) -> p a t", a=16),
                                scalar=sgb_col(j, part * 6 + cl),
                                in1=bbp[:, h, 0:8].unsqueeze(1).to_broadcast([128, 16, 8]),
                                op0=ALU.mult, op1=ALU.add),
                                reads=pk(bk) + ["bbp", "colp", "BIGL"], writes=[("B2", variant)])
            if part == 0:
                issue_wu(0)
            for cl in range(6):
                cg = part * 6 + cl
                s = cg % 8
                for (g0, gs) in sgd.groups:
                    b = sg_ctr[0] % 2
                    sg_ctr[0] += 1
                    ub = 2 + b

                    def _m(e, s=s, g0=g0, gs=gs, ub=ub):
                        ins = None
                        for k in range(8):
                            ins = e.matmul(ps[ub][:, 0:gs], lhsT=uslots[:, s, k, :], rhs=xT[:, k, g0:g0 + gs],
                                           start=(k == 0), stop=(k == 7))
                        return ins
                    tiles = range(g0 // 128, (g0 + gs) // 128)
                    P.op("pe", _m, reads=["WRL", ("wu", s)] + [("xT", i) for i in tiles], writes=["ps%d" % ub],
                         cost=8 * gs / 1950.0)
                    P.op("act", lambda e, ub=ub, cl=cl, g0=g0, gs=gs: e.activation(
                        out=uT[:, cl, g0:g0 + gs], in_=ps[ub][:, 0:gs], func=AF.Gelu),
                        reads=["ps%d" % ub, "BIGL"], writes=[("uT", cl)])
            P.dma("pool", "wout", lambda e, part=part: e.dma_start(out=wout[:, :, :],
                                                                  in_=w_out[:, part * 6:(part + 1) * 6, :]),
                  reads=["BIGL"], writes=["swout"])
            if part < 3:
                issue_wu(part + 1)
            def emit_vn(i):
                mb = i % 2
                P.op("act", lambda e, i=i, mb=mb, part=part: e.activation(
                    out=vn[:, mb, :], in_=vraw[:, i, part * 768:(part + 1) * 768], func=AF.Identity,
                    bias=svst[:, i, 39:40], scale=svst[:, i, 38:39]),
                    reads=[("vraw", i), ("srstd", i), ("snmr", i), "BIGL", "PREPL"], writes=[("vn", mb)])

            def mix_gate(i):
                sample = (i >= sgd.NP)
                WT = WTs if sample else WTp
                B2 = B2s if sample else B2p
                mb = i % 2
                for hh in range(2):
                    bk = MBANKS[m_ctr[0] % 3]
                    m_ctr[0] += 1
                    h = part * 2 + hh

                    def _mm(e, mb=mb, WT=WT, bk=bk, hh=hh, h=h):
                        ins = None
                        for q in range(3):
                            cl = hh * 3 + q
                            ins = e.matmul(ps[bk][:, q * 128:(q + 1) * 128], lhsT=vn[:, mb, cl * 128:(cl + 1) * 128],
                                           rhs=WT[:, h, :], start=True, stop=True)
                        return ins
                    P.op("pe", _mm, reads=[("vn", mb), ("WT", sample), "BIGL"], writes=pk(bk))
                    for q in range(3):
                        cl = hh * 3 + q
                        P.op("dve", lambda e, cl=cl, q=q, mb=mb, part=part, B2=B2, bk=bk: e.scalar_tensor_tensor(
                            out=mtmp[:, mb, cl, :], in0=ps[bk][:, q * 128:(q + 1) * 128],
                            scalar=sgg_col(j, part * 6 + cl), in1=B2[:, cl, :], op0=ALU.mult, op1=ALU.add),
                            reads=pk(bk) + [("B2", int(sample)), "colp", "BIGL", "PREPL"], writes=[("mtmp", mb, cl)])
                P.op(GATE_ENG, lambda e, i=i, mb=mb: e.tensor_tensor(
                    out=uT[:, :, i * 128:(i + 1) * 128], in0=mtmp[:, mb, :, :], in1=uT[:, :, i * 128:(i + 1) * 128],
                    op=ALU.mult),
                    reads=[("mtmp", mb, cl) for cl in range(6)] + [("uT", cl) for cl in range(6)] + ["BIGL"],
                    writes=[("uTg", i)])
            def outp(i):
                out_proj(sgd, i, 6, lambda c, i: uT[:, c, i * 128:(i + 1) * 128], lambda c: wout[:, c, :],
                         [("uTg", i), "swout", "BIGL"] + [("uT", cl) for cl in range(6)])
                if part < 3:
                    sc = ALPHA if part == 0 else 1.0
                    for h, yb in enumerate((4, 5)):
                        P.op("dve", lambda e, i=i, h=h, yb=yb, sc=sc: e.scalar_tensor_tensor(
                            out=xres[:, i, h * 512:(h + 1) * 512], in0=xres[:, i, h * 512:(h + 1) * 512],
                            scalar=float(sc), in1=ps[yb][:, :], op0=ALU.mult, op1=ALU.add),
                            reads=[("xres", i), "ps%d" % yb], writes=[("xres", i)])
                else:
                    epilogue(sgd, i, 4, 5, 1.0, False)
            emit_vn(0)
            for i in range(NT):
                if i + 1 < NT:
                    emit_vn(i + 1)
                mix_gate(i)
                if i >= 1:
                    outp(i - 1)
            outp(NT - 1)
            flush_epi()
            if sgd.has_sample and part < 3:
                i = NT - 1
                for q in range(2):
                    f = 2 * part + q
                    P.op("act", lambda e, q=q, f=f, i=i: e.activation(
                        out=sgt[:, q, :], in_=vraw[:, i, f * 512:(f + 1) * 512], func=AF.Identity,
                        bias=svst[:, i, 39:40], scale=svst[:, i, 38:39]),
                        reads=[("vraw", i), ("srstd", i), ("snmr", i), "BIGL"], writes=[("sgt", q)])
                    P.op("dve", lambda e, q=q: e.tensor_tensor(out=sgt[:, q, :], in0=sgt[:, q, :],
                                                               in1=tbuf[:, 0, q * 512:(q + 1) * 512], op=ALU.mult),
                         reads=[("sgt", q), ("t", 0, q)], writes=[("sgt", q), ("t", 0, q)])
                    P.op("dve", lambda e, q=q: e.tensor_tensor(out=sgt[:, q, :], in0=sgt[:, q, :],
                                                               in1=tbuf[:, 1, q * 512:(q + 1) * 512], op=ALU.add),
                         reads=[("sgt", q), ("t", 1, q)], writes=[("sgt", q), ("t", 1, q)])
                    P.dma("sp", "svo%d" % q, lambda e, q=q, f=f: e.dma_start(out=st_sv[j, :, f * 512:(f + 1) * 512],
                                                                       in_=sgt[:, q, :]),
                          reads=[("sgt", q)])

    sgA = SG("A", list(range(0, 8)), False, [(0, 512), (512, 512)])
    sgB = SG("B", list(range(8, 16)), True, [(0, 384), (384, 384), (768, 384)])
    n_sub_total = DEPTH * 3
    for sgd in (sgA, sgB):
        for i in range(sgd.NT):
            if i < sgd.NP:
                g = sgd.ptiles[i]
                src = xp[g * 128:(g + 1) * 128, :]
            else:
                src = xs[:, :]
            P.dma("sp", "xin%d" % i, lambda e, i=i, src=src: e.dma_start(out=xres[:, i, :], in_=src),
                  writes=[("xres", i)])
            emit_xT(i)
        nsub = 0
        stop = False
        for l in range(DEPTH):
            for sub in range(3):
                if DEBUG_STOP is not None and nsub >= DEBUG_STOP:
                    stop = True
                    break
                is_last = (l == DEPTH - 1 and sub == 2)
                P.fence("BIGL")
                P.fence("WRL")
                if sub == 0:
                    ffn(sgd, l, 0, False)
                elif sub == 1:
                    if l % 2 == 0:
                        conv(sgd, l)
                    else:
                        sgu(sgd, l)
                else:
                    ffn(sgd, l, 1, is_last)
                nsub += 1
            if stop:
                break
        if stop:
            for i in range(sgd.NT):
                P.dma("sp", "yo%d" % i, lambda e, i=i, sgd=sgd: e.dma_start(out=out_rows(sgd, i), in_=xres[:, i, :]),
                      reads=[("xres", i)])
    P.flush()
    for sname in sorted(P.is_dma_sem):
        if sname.startswith(("yo", "svo", "sto")):
            P.final_wait("sp", sname)
    P.emit()
    stack.close()
    return nc


_CONSTS = None


def _consts():
    global _CONSTS
    if _CONSTS is None:
        t = np.arange(128)
        ident = np.eye(128, dtype=np.float32)
        tril = (t[None, :] <= t[:, None]).astype(np.float32)
        blk = ((t[None, :] // 8 == t[:, None] // 8) & (t[None, :] <= t[:, None])).astype(np.float32)
        _CONSTS = (ident, tril, blk)
    return _CONSTS


def kernel(x_prompt, x_sample, cache_conv, ln_g, ln_b, ffn1_w_in, ffn1_w_out, ffn2_w_in, ffn2_w_out,
           conv_w_in, conv_w, conv_w_out, sgu_w_in, sgu_ln_g, sgu_ln_b, sgu_w_s, sgu_b_s, sgu_w_out):
    f = lambda a: np.ascontiguousarray(np.asarray(a, dtype=np.float32))
    x_prompt, x_sample, cache_conv = f(x_prompt), f(x_sample), f(cache_conv)
    shared = {
        "ln_g": f(ln_g), "ln_b": f(ln_b), "ffn1_w_in": f(ffn1_w_in), "ffn1_w_out": f(ffn1_w_out),
        "ffn2_w_in": f(ffn2_w_in), "ffn2_w_out": f(ffn2_w_out), "conv_w_in": f(conv_w_in), "conv_w": f(conv_w),
        "conv_w_out": f(conv_w_out), "sgu_w_in": f(sgu_w_in), "sgu_ln_g": f(sgu_ln_g), "sgu_ln_b": f(sgu_ln_b),
        "sgu_w_s": f(sgu_w_s), "sgu_b_s": f(sgu_b_s), "sgu_w_out": f(sgu_w_out),
    }
    ident, tril, blk = _consts()
    shared.update({"c_ident": ident, "c_tril": tril, "c_blk": blk})
    in_maps = []
    for c in range(NCORES):
        m = dict(shared)
        m["xp"] = x_prompt[c]
        m["xs"] = np.ascontiguousarray(x_sample[16 * c:16 * (c + 1)].reshape(128, D))
        m["cache"] = np.ascontiguousarray(cache_conv[:, 16 * c:16 * (c + 1)].reshape(2, 32, D))
        in_maps.append(m)
    nc = build_nc()
    res = run_bass_kernel_spmd(nc, in_maps, core_ids=list(range(NCORES)))
    r = res.results
    y_p = np.stack([r[c]["y_p"] for c in range(NCORES)], axis=0).reshape(8, 2048, D)
    y_s = np.concatenate([r[c]["y_s"].reshape(16, 8, D) for c in range(NCORES)], axis=0)
    st_cp = np.stack([r[c]["st_cp"] for c in range(NCORES)], axis=1).reshape(2, 8, 2, D)
    st_cs = np.concatenate([r[c]["st_cs"].reshape(2, 16, 2, D) for c in range(NCORES)], axis=1)
    st_sv = np.concatenate([r[c]["st_sv"].reshape(2, 16, 8, DSGU) for c in range(NCORES)], axis=1)
    return (y_p.astype(np.float32), y_s.astype(np.float32), st_cp.astype(np.float32),
            st_cs.astype(np.float32), st_sv.astype(np.float32))
```
